# Optimizing a Trainium2 kernel written in Bass

```python
import jax, jax.numpy as jnp
from jax import lax
import numpy as np

D_MODEL = 1024
BATCH = 4
SEQ = 8192
DEPTH = 1

CONV_WIDTH = 512
CONV_K = 3
ATTN_GROUPS = ((128, 1), (512, 4), (2048, 16))
N_GROUPS = 3
HEADS_PER_GROUP = 4
N_HEADS = N_GROUPS * HEADS_PER_GROUP
HEAD_DIM = 64
ATTN_WIDTH = N_HEADS * HEAD_DIM
ATTN_BLOCK = 128
D_FF = 2816
FFN_K = 3
EPS = 1e-6
NEG_INF = -1e30
IN_SIZES = (CONV_WIDTH, CONV_WIDTH, CONV_WIDTH, ATTN_WIDTH, ATTN_WIDTH, ATTN_WIDTH, D_MODEL, D_MODEL)
D_IN = sum(IN_SIZES)

kernel_name = "hybrid_shortconv_dilated_swa_convffn"


def rms_norm(x, g):
    xf = x.astype(jnp.float32)
    y = xf * lax.rsqrt(jnp.mean(xf * xf, axis=-1, keepdims=True) + EPS)
    return (y * g.astype(jnp.float32)).astype(x.dtype)


def causal_dwconv(x, w, b):
    K, C = w.shape
    y = lax.conv_general_dilated(
        x, w[:, None, :], window_strides=(1,), padding=[(K - 1, 0)],
        dimension_numbers=("NWC", "WIO", "NWC"), feature_group_count=C)
    return y + b


def dilated_window_attention(q, k, v, window, dilation):
    B, S, H, dh = q.shape
    Q = ATTN_BLOCK
    n_back = window // dilation
    L = S // dilation
    Lp = L + (-L) % Q
    nb = Lp // Q

    def to_streams(t):
        t = t.reshape(B, L, dilation, H, dh).transpose(0, 2, 1, 3, 4)
        return jnp.pad(t, ((0, 0), (0, 0), (0, Lp - L), (0, 0), (0, 0)))

    qs, ks, vs = to_streams(q), to_streams(k), to_streams(v)
    qb = qs.reshape(B, dilation, nb, Q, H, dh)

    def band_blocks(t):
        tp = jnp.pad(t, ((0, 0), (0, 0), (Q, 0), (0, 0), (0, 0)))
        prev = tp[:, :, :Lp].reshape(B, dilation, nb, Q, H, dh)
        cur = t.reshape(B, dilation, nb, Q, H, dh)
        return jnp.concatenate([prev, cur], axis=3)

    kb, vb = band_blocks(ks), band_blocks(vs)
    scores = jnp.einsum("brnqhe,brnkhe->brnhqk", qb, kb,
                        preferred_element_type=jnp.float32) * (dh ** -0.5)
    qi = jnp.arange(Q)[:, None]
    kj = jnp.arange(2 * Q)[None, :]
    dist = qi + Q - kj
    key_pos = jnp.arange(nb)[:, None, None] * Q + kj - Q
    valid = (dist >= 0) & (dist <= n_back) & (key_pos >= 0)
    scores = jnp.where(valid[:, None], scores, NEG_INF)
    m = jnp.max(scores, axis=-1)
    p = jnp.exp(scores - m[..., None])
    l = jnp.sum(p, axis=-1)
    o = jnp.einsum("brnhqk,brnkhe->brnqhe", p.astype(v.dtype), vb,
                   preferred_element_type=jnp.float32)
    o = o / jnp.swapaxes(l, 3, 4)[..., None]

    def from_streams(t):
        t = t.reshape((B, dilation, Lp) + t.shape[4:])[:, :, :L]
        t = jnp.moveaxis(t, 1, 2)
        return t.reshape((B, S) + t.shape[3:])

    return (from_streams(o), from_streams(jnp.swapaxes(m, 3, 4)),
            from_streams(jnp.swapaxes(l, 3, 4)))


def setup_inputs(seed: int = 0) -> dict:
    key = jax.random.key(seed)
    ks = jax.random.split(key, 16)
    f32 = jnp.float32

    def nrm(k, shape, scale):
        return jax.random.normal(k, shape, f32) * scale

    return {
        "x": nrm(ks[0], (BATCH, SEQ, D_MODEL), 1.0),
        "norm_mix_g": 1.0 + nrm(ks[1], (DEPTH, D_MODEL), 0.02),
        "w_in": nrm(ks[2], (DEPTH, D_MODEL, D_IN), D_MODEL ** -0.5),
        "b_gate": nrm(ks[3], (DEPTH, 2, D_MODEL), 0.01),
        "conv_a_w": nrm(ks[4], (DEPTH, CONV_K, CONV_WIDTH), CONV_K ** -0.5),
        "conv_a_b": nrm(ks[5], (DEPTH, CONV_WIDTH), 0.01),
        "w_proj_a": nrm(ks[6], (DEPTH, CONV_WIDTH, D_MODEL), CONV_WIDTH ** -0.5),
        "w_proj_b": nrm(ks[7], (DEPTH, ATTN_WIDTH, D_MODEL), ATTN_WIDTH ** -0.5),
        "w_out": nrm(ks[8], (DEPTH, D_MODEL, D_MODEL), D_MODEL ** -0.5),
        "norm_ffn_g": 1.0 + nrm(ks[9], (DEPTH, D_MODEL), 0.02),
        "w_up": nrm(ks[10], (DEPTH, D_MODEL, 2 * D_FF), D_MODEL ** -0.5),
        "ffn_conv_w": nrm(ks[11], (DEPTH, FFN_K, 2 * D_FF), FFN_K ** -0.5),
        "ffn_conv_b": nrm(ks[12], (DEPTH, 2 * D_FF), 0.01),
        "w_down": nrm(ks[13], (DEPTH, D_FF, D_MODEL), D_FF ** -0.5),
        "final_norm_g": 1.0 + nrm(ks[14], (D_MODEL,), 0.02),
    }


def reference(x, norm_mix_g, w_in, b_gate, conv_a_w, conv_a_b, w_proj_a, w_proj_b,
              w_out, norm_ffn_g, w_up, ffn_conv_w, ffn_conv_b, w_down, final_norm_g):
    B, S, _ = x.shape
    split_points = [int(s) for s in np.cumsum(IN_SIZES)[:-1]]
    for layer in range(DEPTH):
        h = rms_norm(x, norm_mix_g[layer])
        proj = h @ w_in[layer]
        a_b, a_c, a_v, q, k, v, g_a, g_b = jnp.split(proj, split_points, axis=-1)

        y_a = a_b * causal_dwconv(a_c * a_v, conv_a_w[layer], conv_a_b[layer])
        y_a = y_a @ w_proj_a[layer]

        q = q.reshape(B, S, N_GROUPS, HEADS_PER_GROUP, HEAD_DIM)
        k = k.reshape(B, S, N_GROUPS, HEADS_PER_GROUP, HEAD_DIM)
        v = v.reshape(B, S, N_GROUPS, HEADS_PER_GROUP, HEAD_DIM)
        outs, ms, ls = [], [], []
        for gi, (window, dilation) in enumerate(ATTN_GROUPS):
            o_g, m_g, l_g = dilated_window_attention(
                q[:, :, gi], k[:, :, gi], v[:, :, gi], window, dilation)
            outs.append(o_g)
            ms.append(m_g)
            ls.append(l_g)
        m_all = jnp.stack(ms, axis=2)
        l_all = jnp.stack(ls, axis=2)
        o_all = jnp.stack(outs, axis=2)
        w_den = l_all * jnp.exp(m_all - jnp.max(m_all, axis=2, keepdims=True))
        alpha = w_den / jnp.sum(w_den, axis=2, keepdims=True)
        y_b = (alpha[..., None] * o_all).reshape(B, S, ATTN_WIDTH).astype(x.dtype)
        y_b = y_b @ w_proj_b[layer]

        merged = (jax.nn.sigmoid(g_a + b_gate[layer, 0]) * y_a
                  + jax.nn.sigmoid(g_b + b_gate[layer, 1]) * y_b)
        x = x + merged @ w_out[layer]

        h = rms_norm(x, norm_ffn_g[layer])
        up = causal_dwconv(h @ w_up[layer], ffn_conv_w[layer], ffn_conv_b[layer])
        gate, val = jnp.split(up, 2, axis=-1)
        x = x + (jax.nn.silu(gate) * val) @ w_down[layer]
    return rms_norm(x, final_norm_g)
```

```python
import numpy as np
from contextlib import ExitStack
import concourse.bass as bass
import concourse.mybir as mybir
from concourse.bass_utils import run_bass_kernel_spmd

F32 = mybir.dt.float32
BF16 = mybir.dt.bfloat16
AF = mybir.ActivationFunctionType
ALU = mybir.AluOpType

D = 1024
KC = 8
DIN = 5888
DFF = 2816
NLOC = 8192
OUT0 = 2176
NOUT = 4096
NT2 = 9
NQ2 = 4224
EPS = 1e-6
SEM_LIMIT = 24000
NWS = 4

COMPUTE = ("pe", "act", "dve", "pool")


class Op:
    __slots__ = ("eng", "fn", "deps", "signal", "ev", "dma_sem", "idx")

    def __init__(self, eng, fn, dma_sem=None):
        self.eng = eng
        self.fn = fn
        self.deps = []
        self.signal = False
        self.ev = None
        self.dma_sem = dma_sem


class Sched:
    def __init__(self):
        self.ops = {e: [] for e in ("pe", "act", "dve", "pool", "sp")}
        self.last_w = {}
        self.readers = {}
        self.dma_count = {}
        self.n = 0
        self.pending_barrier = {}
        self.barrier_dmas = []

    def barrier(self):
        lasts = [q[-1] for q in self.ops.values() if q]
        lasts += self.barrier_dmas
        self.barrier_dmas = []
        self.pending_barrier = {q: list(lasts) for q in self.ops}

    def add(self, eng, fn, reads=(), writes=(), dma_sem=None, barrier_dma=False):
        q = {"gq": "pool", "aq": "act"}.get(eng, eng)
        op = Op(eng, fn, dma_sem)
        op.idx = self.n
        self.n += 1
        isdma = dma_sem is not None
        deps = {}
        for r in reads:
            w = self.last_w.get(r)
            if w is not None:
                deps[id(w)] = w
        for w_ in writes:
            lw = self.last_w.get(w_)
            if lw is not None:
                deps[id(lw)] = lw
            for rd in self.readers.get(w_, {}).values():
                deps[id(rd)] = rd
        if q in self.pending_barrier:
            for d in self.pending_barrier.pop(q):
                deps[id(d)] = d
        for d in deps.values():
            if (not isdma) and d.dma_sem is None and d.eng == eng == "pe":
                continue
            op.deps.append(d)
            d.signal = True
        for r in reads:
            rd = self.readers.setdefault(r, {})
            rd[("dma", op.idx) if isdma else eng] = op
        for w_ in writes:
            self.last_w[w_] = op
            self.readers[w_] = {}
        if isdma:
            c = self.dma_count.get(dma_sem, 0) + 16
            self.dma_count[dma_sem] = c
            op.ev = (dma_sem, c)
            if barrier_dma:
                self.barrier_dmas.append(op)
        self.ops[q].append(op)
        return op


class Arena:
    def __init__(self, ap):
        self.ap = ap
        self.off = 0
        self.hi = 0

    def reset(self):
        self.hi = max(self.hi, self.off)
        self.off = 0

    def alloc(self, shape, dt):
        n = 1
        for s_ in shape[1:]:
            n *= s_
        nb = n * (4 if dt == F32 else 2)
        nb = (nb + 63) // 64 * 64
        v = self.ap[:, self.off // 2:(self.off + nb) // 2]
        self.off += nb
        self.hi = max(self.hi, self.off)
        if dt == F32:
            v = v.bitcast(F32)
        v = v[:, 0:n]
        if len(shape) == 3:
            v = v.rearrange("p (a b) -> p a b", a=shape[1])
        elif len(shape) == 4:
            v = v.rearrange("p (a b c) -> p a b c", a=shape[1], b=shape[2])
        return v


ARENA_BYTES = 176 * 1024


def build_program(dbg=None):
    nc = bass.Bass("TRN2", target_bir_lowering=False)
    S = Sched()
    dbg = dbg or {}

    def din(name, shape, dt=F32):
        return nc.dram_tensor(name, list(shape), dt, kind="ExternalInput").ap()

    x_loc = din("x_loc", [NLOC, D])
    w_in = din("w_in", [D, DIN])
    w_pa = din("w_pa", [512, D])
    w_pb = din("w_pb", [768, D])
    w_out = din("w_out", [D, D])
    w_up = din("w_up", [D, 2 * DFF])
    w_dn = din("w_dn", [DFF, D])
    g1_d = din("g1", [1, D])
    g2_d = din("g2", [1, D])
    g3_d = din("g3", [1, D])
    bg_d = din("bg", [128, 16])
    caw_d = din("caw", [128, 12])
    cab_d = din("cab", [128, 4])
    fcw_d = din("fcw", [128, 132])
    fcb_d = din("fcb", [128, 44])
    vld0_d = din("vld0", [128, 40])
    vld1_d = din("vld1", [128, 40])
    vld2_d = din("vld2", [128, 64])
    perm_d = din("perm16", [128, 128])
    out_d = nc.dram_tensor("out", [NOUT, D], F32, kind="ExternalOutput").ap()

    def dscr(name, shape, dt=BF16):
        return nc.dram_tensor(name, list(shape), dt, kind="Internal").ap()

    win_b = dscr("win_b", [D, DIN])
    wpa_b = dscr("wpa_b", [512, D])
    wpb_b = dscr("wpb_b", [768, D])
    wout_b = dscr("wout_b", [D, D])
    wup_b = dscr("wup_b", [D, 2 * DFF])
    wdn_b = dscr("wdn_b", [DFF, D])
    u2_d = dscr("u2_d", [128, 2, 6144])
    l2_d = dscr("l2_d", [128, 2, 6144], F32)

    dbg_outs = {}
    for k, shp in dbg.items():
        if k.startswith("_"):
            continue
        dbg_outs[k] = nc.dram_tensor("dbg_" + k, list(shp[0]), BF16 if shp[1] == "bf16" else F32,
                                     kind="ExternalOutput").ap()

    es = ExitStack()
    sb_bytes = [0]

    def sb(name, shape, dt=F32):
        n = 1
        for s_ in shape[1:]:
            n *= s_
        sb_bytes[0] += n * (4 if dt == F32 else 2)
        return es.enter_context(nc.sbuf_tensor(name, list(shape), dt))[:]

    def newsem(name):
        return es.enter_context(nc.semaphore(name))

    ident = sb("ident", [128, 128], BF16)
    mask4 = sb("mask4", [128, 512], BF16)
    ones3 = sb("ones3", [128, 4, 64], BF16)
    eps_t = sb("eps_t", [128, 1], F32)
    g1b = sb("g1b", [128, D], F32)
    g2b = sb("g2b", [128, D], F32)
    g3b = sb("g3b", [128, D], F32)
    bg = sb("bg_s", [128, 16], F32)
    caw = sb("caw_s", [128, 12], F32)
    cab = sb("cab_s", [128, 4], F32)
    fcw = sb("fcw_s", [128, 132], F32)
    fcb = sb("fcb_s", [128, 44], F32)
    vld0 = sb("vld0_s", [128, 40], F32)
    vld1 = sb("vld1_s", [128, 40], F32)
    vld2 = sb("vld2_s", [128, 64], F32)
    perm16 = sb("perm16_s", [128, 128], BF16)
    ssq = sb("ssq", [128, 12], F32)
    srt = sb("srt", [128, 12], F32)
    rstd = sb("rstd", [128, 12], F32)
    junk = sb("junk", [128, D], BF16)
    hb = [sb("hb%d" % i, [128, D], BF16) for i in range(2)]
    PT = [sb("PT%d" % i, [128, 512], BF16) for i in range(4)]
    uc = sb("uc", [128, 2, 44, 2], F32)
    arena_t = sb("arena", [128, ARENA_BYTES // 2], BF16)
    A = Arena(arena_t)

    ps = [es.enter_context(nc.psum_tensor("ps%d" % i, [128, 512], F32))[:] for i in range(8)]
    tpv = [p.bitcast(BF16).rearrange("p (k c) -> p k c", k=8) for p in ps]
    ps_rr = [0]

    def next_ps():
        b = ps_rr[0] % 8
        ps_rr[0] += 1
        return b

    sem_const = newsem("s_const")

    const_ops = []

    def cdma(dst, src, key):
        const_ops.append(S.add("sp", lambda e: e.dma_start(out=dst, in_=src), writes=[key], dma_sem=sem_const))

    cdma(g1b, g1_d.partition_broadcast(128), "g1b")
    cdma(g2b, g2_d.partition_broadcast(128), "g2b")
    cdma(g3b, g3_d.partition_broadcast(128), "g3b")
    cdma(bg, bg_d[:, :], "bg")
    cdma(caw, caw_d[:, :], "caw")
    cdma(cab, cab_d[:, :], "cab")
    cdma(fcw, fcw_d[:, :], "fcw")
    cdma(fcb, fcb_d[:, :], "fcb")
    cdma(vld0, vld0_d[:, :], "vld0")
    cdma(vld1, vld1_d[:, :], "vld1")
    cdma(vld2, vld2_d[:, :], "vld2")
    for o_ in const_ops:
        o_.ev = const_ops[-1].ev

    identf = A.alloc([128, 128], F32)
    maskf = A.alloc([128, 256], F32)
    xs = [A.alloc([128, D], F32) for i in range(8)]
    hT2 = A.alloc([128, KC, 2048], BF16)
    W2 = A.alloc([128, KC, 768], BF16)
    K2T = [A.alloc([128, 2, 2048], BF16) for i in range(2)]
    Q2T = A.alloc([128, 2, 2048], BF16)
    V2A = [A.alloc([128, 16, 4, 128], BF16) for i in range(2)]
    U2s = A.alloc([128, 2, 2048], BF16)
    L2s = A.alloc([128, 2, 2048], F32)
    NCS = 3
    cst = [A.alloc([128, 1024], F32) for i in range(NCS)]
    csb = [A.alloc([128, 1024], BF16) for i in range(NCS)]
    sem_cst = [newsem("s_cst%d" % i) for i in range(NCS)]
    sem_cso = newsem("s_cso")
    print("phase-1 arena bytes:", A.off)
    sem_xs = [newsem("s_xs%d" % i) for i in range(8)]
    sem_w2 = newsem("s_w2")
    sem_ul = newsem("s_ul")

    S.add("pool", lambda e: e.memset(identf, 0.0), writes=["identf"])
    S.add("pool", lambda e: e.affine_select(out=identf, in_=identf, pattern=[[-1, 128]],
                                            compare_op=ALU.not_equal, fill=1.0, base=0,
                                            channel_multiplier=1), reads=["identf"], writes=["identf"])
    S.add("dve", lambda e: e.tensor_copy(out=ident, in_=identf), reads=["identf"], writes=["ident"])
    S.add("sp", lambda e: e.dma_start(out=identf, in_=perm_d[:, :]), reads=["ident"], writes=["identf"], dma_sem=sem_w2)
    S.add("dve", lambda e: e.tensor_copy(out=perm16, in_=identf), reads=["identf"], writes=["perm16"])
    S.add("pool", lambda e: e.memset(maskf, 1.0), writes=["maskf"])
    S.add("pool", lambda e: e.affine_select(out=maskf[:, 0:128], in_=maskf[:, 0:128], pattern=[[1, 128]],
                                            compare_op=ALU.is_ge, fill=0.0, base=0,
                                            channel_multiplier=-1), reads=["maskf"], writes=["maskf"])
    S.add("pool", lambda e: e.affine_select(out=maskf[:, 128:256], in_=maskf[:, 128:256], pattern=[[-1, 128]],
                                            compare_op=ALU.is_ge, fill=0.0, base=0,
                                            channel_multiplier=1), reads=["maskf"], writes=["maskf"])
    S.add("dve", lambda e: e.tensor_copy(out=mask4[:, 0:256], in_=maskf), reads=["maskf"], writes=["mask4"])
    S.add("dve", lambda e: e.tensor_copy(out=mask4[:, 256:512], in_=maskf), reads=["maskf"], writes=["mask4"])
    S.add("pool", lambda e: e.memset(ones3, 1.0), writes=["ones3"])
    S.add("pool", lambda e: e.memset(eps_t, EPS), writes=["eps"])
    S.add("pool", lambda e: e.memset(uc, 0.0), writes=[("uc", p_, ch_) for p_ in range(2) for ch_ in range(44)])

    def norm_stats(src_rows, nsub, sq_base, rkey):
        for s_ in range(nsub):
            col = sq_base + s_
            S.add("act", lambda e, s_=s_, col=col: e.activation(
                out=junk, in_=src_rows(s_), func=AF.Square, accum_out=ssq[:, col:col + 1]),
                reads=[rkey(s_)], writes=[("ssq", col)])
        cols = slice(sq_base, sq_base + nsub)
        S.add("act", lambda e: e.activation(out=srt[:, cols], in_=ssq[:, cols], func=AF.Sqrt,
                                            scale=1.0 / D, bias=eps_t),
              reads=[("ssq", c) for c in range(sq_base, sq_base + nsub)] + ["eps"],
              writes=[("srt", sq_base)])
        S.add("dve", lambda e: e.reciprocal(out=rstd[:, cols], in_=srt[:, cols]),
              reads=[("srt", sq_base)], writes=[("rstd", c) for c in range(sq_base, sq_base + nsub)])

    hb_rr = [0]

    def norm_transpose(src_rows, nsub, gb, gkey, sq_base, rkey, dst_fn, src_view, dst_keys):
        for s_ in range(nsub):
            i = hb_rr[0] % 2
            hb_rr[0] += 1
            col = sq_base + s_
            S.add("dve", lambda e, s_=s_, i=i, col=col: e.scalar_tensor_tensor(
                out=hb[i], in0=src_rows(s_), scalar=rstd[:, col:col + 1], in1=gb,
                op0=ALU.mult, op1=ALU.mult),
                reads=[rkey(s_), ("rstd", col), gkey], writes=[("hb", i)])
            t = next_ps()
            for kc in range(KC):
                S.add("pe", lambda e, i=i, t=t, kc=kc: e.transpose(
                    out=tpv[t][:, kc, :], in_=hb[i][:, kc * 128:(kc + 1) * 128], identity=ident),
                    reads=[("hb", i), "ident"], writes=[("ps", t)])
            S.add("act", lambda e, s_=s_, t=t: e.activation(out=dst_fn(s_), in_=src_view(tpv[t]), func=AF.Copy),
                  reads=[("ps", t)], writes=dst_keys(s_))

    def attention_run(calls, pt_rr, after_call=None):
        steps = [(ci, q2) for ci, c in enumerate(calls) for q2 in range(0, c["nblk"], 2)]
        st = {}

        def stage_a(k):
            ci, q2 = steps[k]
            c = calls[ci]
            bs = [next_ps(), next_ps()]
            nq = min(2, c["nblk"] - q2)
            for qq in range(nq):
                q = q2 + qq
                for kp, Kf in enumerate((c["Kcur"], c["Kprev"])):
                    for h in range(2):
                        S.add("pe", lambda e, b=bs[h], qq=qq, kp=kp, Kf=Kf, q=q, h=h, Qf=c["Qblk"]: e.matmul(
                            ps[b][:, qq * 256 + kp * 128: qq * 256 + kp * 128 + 128],
                            lhsT=Kf(q, h), rhs=Qf(q, h), start=True, stop=True, skip_group_check=True),
                            reads=c["kkeys"](q), writes=[("ps", bs[h])])
            pis = []
            for h in range(2):
                pi = pt_rr[0] % 4
                pt_rr[0] += 1
                pis.append(pi)
                w = nq * 256
                S.add("act", lambda e, b=bs[h], pi=pi, w=w: e.activation(
                    out=PT[pi][:, 0:w], in_=ps[b][:, 0:w], func=AF.Exp, scale=0.125),
                    reads=[("ps", bs[h])], writes=[("PT", pi)])
                S.add("dve", lambda e, pi=pi, w=w: e.tensor_tensor(
                    out=PT[pi][:, 0:w], in0=PT[pi][:, 0:w], in1=mask4[:, 0:w], op=ALU.mult),
                    reads=[("PT", pi), "mask4"], writes=[("PT", pi)])
            st[k] = (pis, nq)

        def stage_b(k):
            ci, q2 = steps[k]
            c = calls[ci]
            if "bo" not in c:
                c["bo"] = [next_ps(), next_ps()]
                c["first"] = [True, True]
            bo = c["bo"]
            pis, nq = st.pop(k)
            for qq in range(nq):
                q = q2 + qq
                for h in range(2):
                    for kp, Vf in enumerate((c["Vcur"], c["Vprev"])):
                        S.add("pe", lambda e, b=bo[h], q=q, qq=qq, h=h, kp=kp, Vf=Vf, pi=pis[h], st_=c["first"][h]: e.matmul(
                            ps[b][:, q * 128:(q + 1) * 128], lhsT=Vf(q, h),
                            rhs=PT[pi][:, qq * 256 + kp * 128: qq * 256 + kp * 128 + 128],
                            start=st_, stop=(kp == 1), skip_group_check=True),
                            reads=c["vkeys"](q) + [("PT", pis[h])], writes=[("ps", bo[h])])
                        c["first"][h] = False
            if q2 + 2 >= c["nblk"]:
                c["evac"](bo)
                if after_call is not None:
                    after_call(ci)

        if not steps:
            return
        stage_a(0)
        for k in range(len(steps)):
            if k + 1 < len(steps):
                stage_a(k + 1)
            stage_b(k)

    cs_rr = [0]
    cs_busy = [False] * NCS

    def cast_slot():
        i = cs_rr[0] % NCS
        cs_rr[0] += 1
        assert not cs_busy[i], "cast staging ring overflow"
        cs_busy[i] = True
        return i

    for kc in range(KC):
        i = cast_slot()
        ops_ = [S.add("sp", lambda e, i=i, kc=kc, a=a, c0=c0: e.dma_start(
            out=cst[i][:, a * 256:(a + 1) * 256], in_=w_in[kc * 128:(kc + 1) * 128, c0:c0 + 256]),
            reads=[("cst", i)], writes=[("cst", i, a)], dma_sem=sem_cst[i]) for a, c0 in enumerate((2048, 2816, 3584))]
        for o_ in ops_:
            o_.ev = ops_[-1].ev
        S.add("pool", lambda e, i=i, kc=kc: e.tensor_copy(out=W2[:, kc, :], in_=cst[i][:, 0:768]),
              reads=[("cst", i, 0), ("cst", i, 1), ("cst", i, 2)], writes=["W2", ("cst", i)])
        cs_busy[i] = False

    def cast_chunks():
        for (src, dst, K_, N_) in ((w_in, win_b, D, DIN), (w_pa, wpa_b, 512, D), (w_pb, wpb_b, 768, D),
                                   (w_out, wout_b, D, D), (w_up, wup_b, D, 2 * DFF), (w_dn, wdn_b, DFF, D)):
            for r0 in range(0, K_, 128):
                for c0 in range(0, N_, 1024):
                    n = min(1024, N_ - c0)
                    yield src[r0:r0 + 128, c0:c0 + n], dst[r0:r0 + 128, c0:c0 + n], n

    cast_iter = cast_chunks()
    cast_pending = []

    def cast_step(nload, keep=1):
        while len(cast_pending) > keep:
            i, dst, n = cast_pending.pop(0)
            S.add("sp", lambda e, i=i, dst=dst, n=n: e.dma_start(out=dst, in_=csb[i][:, 0:n]),
                  reads=[("csb", i)], dma_sem=sem_cso, barrier_dma=True)
            cs_busy[i] = False
        for _ in range(nload):
            nxt = next(cast_iter, None)
            if nxt is None:
                break
            src, dst, n = nxt
            i = cast_slot()
            S.add("sp", lambda e, i=i, src=src, n=n: e.dma_start(out=cst[i][:, 0:n], in_=src),
                  writes=[("cst", i)], dma_sem=sem_cst[i])
            S.add("pool", lambda e, i=i, n=n: e.tensor_copy(out=csb[i][:, 0:n], in_=cst[i][:, 0:n]),
                  reads=[("cst", i)], writes=[("csb", i)])
            cast_pending.append((i, dst, n))

    for _ in range(200):
        cast_step(2)
    cast_step(0, keep=0)
    assert next(cast_iter, None) is None and not cast_pending
    xs_rr = [0]
    pt_rr = [0]
    pre_q = []
    norm_done = set()
    nsb = dbg.get("_nsb", 4)
    for s in range(nsb):
        par = s % 2
        def p1_load(qt, sb_=None):
            sb_ = s if sb_ is None else sb_
            slots = []
            for sub in range(4):
                i = xs_rr[0] % 8
                xs_rr[0] += 1
                slots.append(i)
                tok0 = sb_ * 2048 + qt * 512 + sub * 128
                S.add("aq", lambda e, i=i, tok0=tok0: e.dma_start(out=xs[i], in_=x_loc[tok0:tok0 + 128, :]),
                      writes=[("xs", i)], dma_sem=sem_xs[i])
            return slots

        def p1_stats(qt, slots):
            norm_stats(lambda s_, slots=slots: xs[slots[s_]], 4, (qt % 2) * 4, lambda s_, slots=slots: ("xs", slots[s_]))

        def p1_nt(qt, slots, nsub=4):
            sqb = (qt % 2) * 4
            for s_ in range(nsub):
                i = hb_rr[0] % 2
                hb_rr[0] += 1
                col = sqb + s_
                sl = slots[s_]
                S.add("dve", lambda e, i=i, col=col, sl=sl: e.scalar_tensor_tensor(
                    out=hb[i], in0=xs[sl], scalar=rstd[:, col:col + 1], in1=g1b, op0=ALU.mult, op1=ALU.mult),
                    reads=[("xs", sl), ("rstd", col), "g1b"], writes=[("hb", i)])
                j0 = qt * 32 + s_ * 8
                for half in range(2):
                    t = next_ps()
                    for k4 in range(4):
                        kc = half * 4 + k4
                        S.add("pe", lambda e, i=i, t=t, kc=kc, k4=k4: e.matmul(
                            ps[t][:, k4 * 128:(k4 + 1) * 128], lhsT=hb[i][:, kc * 128:(kc + 1) * 128], rhs=perm16,
                            start=True, stop=True, skip_group_check=True),
                            reads=[("hb", i), "perm16"], writes=[("ps", t)])
                    dst = hT2[:, half * 4:half * 4 + 4, :].rearrange("p k (r j) -> p k r j", r=16)[:, :, :, j0:j0 + 8]
                    src = ps[t].rearrange("p (k r j) -> p k r j", k=4, r=16)
                    if half == 0:
                        S.add("act", lambda e, dst=dst, src=src: e.activation(out=dst, in_=src, func=AF.Copy),
                              reads=[("ps", t)], writes=[("hT2", qt, 0)])
                    else:
                        S.add("dve", lambda e, dst=dst, src=src: e.tensor_copy(out=dst, in_=src),
                              reads=[("ps", t)], writes=[("hT2", qt, 1)])

        if s == 3:
            i = xs_rr[0] % 8
            xs_rr[0] += 1
            S.add("aq", lambda e, i=i: e.dma_start(out=xs[i], in_=x_loc[6144:6272, :]),
                  writes=[("xs", i)], dma_sem=sem_xs[i])
            norm_stats(lambda s_, i=i: xs[i], 1, 0, lambda s_, i=i: ("xs", i))
            p1_nt(0, [i], nsub=1)
            hT2_keys = [("hT2", 0, 0), ("hT2", 0, 1)]
            hv = lambda t: t.rearrange("p (r j) -> p r j", r=16)[:, :, 0:8]
            for pair in range(2):
                for (dst, dkey, wc0) in ((K2T[par], ("K2T", par), 256), (Q2T, ("Q2T",), 0)):
                    b = next_ps()
                    for kc in range(KC):
                        S.add("pe", lambda e, b=b, kc=kc, wc0=wc0, pair=pair: e.matmul(
                            ps[b][:, 0:128], lhsT=W2[:, kc, wc0 + pair * 128: wc0 + pair * 128 + 128],
                            rhs=hv(hT2[:, kc, :]), start=(kc == 0), stop=(kc == KC - 1)),
                            reads=["W2"] + hT2_keys, writes=[("ps", b)])
                    S.add("act", lambda e, b=b, dst=dst, pair=pair: e.activation(
                        out=hv(dst[:, pair, :]), in_=ps[b][:, 0:128].rearrange("p (r j) -> p r j", r=16), func=AF.Copy),
                        reads=[("ps", b)], writes=[dkey + (pair, cq) for cq in range(4)])
            for r2 in range(8):
                b = next_ps()
                for rr in range(2):
                    r = r2 * 2 + rr
                    for kc in range(KC):
                        S.add("pe", lambda e, b=b, kc=kc, r=r, rr=rr: e.matmul(
                            ps[b][0:8, rr * 256:(rr + 1) * 256], lhsT=hT2[:, kc, r * 128:r * 128 + 8],
                            rhs=W2[:, kc, 512:768], start=(kc == 0), stop=(kc == KC - 1), skip_group_check=True),
                            reads=["W2"] + hT2_keys, writes=[("ps", b)])
                S.add("act", lambda e, b=b, r2=r2, par=par: e.activation(
                    out=V2A[par][0:8, r2 * 2:r2 * 2 + 2, :, 0:64],
                    in_=ps[b][0:8, :].rearrange("p (r h e) -> p r h e", r=2, h=4), func=AF.Copy),
                    reads=[("ps", b)], writes=[("V2A", par, r2)])
                for rr in range(2):
                    r = r2 * 2 + rr
                    S.add("dve", lambda e, r=r, par=par, s=s: e.tensor_scalar(
                        out=V2A[par][:, r, :, 64:128], in0=ones3, scalar1=vld2[:, s * 16 + r:s * 16 + r + 1],
                        scalar2=None, op0=ALU.mult),
                        reads=["ones3", "vld2"], writes=[("V2A1", par, r)])
        def norm_gen(sv):
            if not pre_q:
                pre_q.extend([p1_load(0, sv), p1_load(1, sv)])
            sl_q = list(pre_q)
            del pre_q[:]
            p1_stats(0, sl_q[0])
            yield
            for qt in range(4):
                if qt + 1 < 4:
                    p1_stats(qt + 1, sl_q[qt + 1])
                p1_nt(qt, sl_q[qt])
                if qt + 2 < 4:
                    sl_q.append(p1_load(qt + 2, sv))
                elif sv + 1 < min(nsb, 3):
                    pre_q.append(p1_load(qt + 2 - 4, sv + 1))
                yield

        if s < 3 and s not in norm_done:
            for _ in norm_gen(s):
                pass
            norm_done.add(s)
        hT2_keys = [("hT2", q, hh_) for q in range(4) for hh_ in range(2)]
        for pair in (range(2) if s < 3 else ()):
            for (dst, dkey, wc0, need) in ((K2T[par], ("K2T", par), 256, True), (Q2T, ("Q2T",), 0, s >= 1)):
                if not need:
                    continue
                for cq in range(4):
                    b = next_ps()
                    for kc in range(KC):
                        S.add("pe", lambda e, b=b, kc=kc, wc0=wc0, pair=pair, cq=cq: e.matmul(
                            ps[b], lhsT=W2[:, kc, wc0 + pair * 128: wc0 + pair * 128 + 128],
                            rhs=hT2[:, kc, cq * 512:(cq + 1) * 512], start=(kc == 0), stop=(kc == KC - 1)),
                            reads=["W2"] + hT2_keys, writes=[("ps", b)])
                    if cq % 2 == 0:
                        S.add("act", lambda e, b=b, dst=dst, pair=pair, cq=cq: e.activation(
                            out=dst[:, pair, cq * 512:(cq + 1) * 512], in_=ps[b], func=AF.Copy),
                            reads=[("ps", b)], writes=[dkey + (pair, cq)])
                    else:
                        S.add("dve", lambda e, b=b, dst=dst, pair=pair, cq=cq: e.tensor_copy(
                            out=dst[:, pair, cq * 512:(cq + 1) * 512], in_=ps[b]),
                            reads=[("ps", b)], writes=[dkey + (pair, cq)])
        for r2 in (range(8) if s < 3 else ()):
            b = next_ps()
            for rr in range(2):
                r = r2 * 2 + rr
                for kc in range(KC):
                    S.add("pe", lambda e, b=b, kc=kc, r=r, rr=rr: e.matmul(
                        ps[b][:, rr * 256:(rr + 1) * 256], lhsT=hT2[:, kc, r * 128:(r + 1) * 128],
                        rhs=W2[:, kc, 512:768], start=(kc == 0), stop=(kc == KC - 1), skip_group_check=True),
                        reads=["W2"] + hT2_keys, writes=[("ps", b)])
            S.add("act", lambda e, b=b, r2=r2, par=par: e.activation(
                out=V2A[par][:, r2 * 2:r2 * 2 + 2, :, 0:64],
                in_=ps[b].rearrange("p (r h e) -> p r h e", r=2, h=4), func=AF.Copy),
                reads=[("ps", b)], writes=[("V2A", par, r2)])
            for rr in range(2):
                r = r2 * 2 + rr
                S.add("dve", lambda e, r=r, par=par, s=s: e.tensor_scalar(
                    out=V2A[par][:, r, :, 64:128], in0=ones3, scalar1=vld2[:, s * 16 + r:s * 16 + r + 1],
                    scalar2=None, op0=ALU.mult),
                    reads=["ones3", "vld2"], writes=[("V2A1", par, r)])
        if s == 0:
            continue
        calls = []
        hsl = lambda h: slice(h * 64, (h + 1) * 64)
        for pair in range(2):
            for r4 in range(4):
                blk = lambda q, r4=r4: slice((r4 * 4 + q) * 128, (r4 * 4 + q + 1) * 128)

                def evac(bo, pair=pair, r4=r4):
                    for h in range(2):
                        hs = slice(h * 64, (h + 1) * 64)
                        src_u = ps[bo[h]][0:64, :].rearrange("p (r j) -> p j r", r=4)
                        src_l = ps[bo[h]][64:128, :].rearrange("p (r j) -> p j r", r=4)
                        du = U2s[hs, pair, :].rearrange("p (j r) -> p j r", r=16)[:, :, r4 * 4:r4 * 4 + 4]
                        dl = L2s[hs, pair, :].rearrange("p (j r) -> p j r", r=16)[:, :, r4 * 4:r4 * 4 + 4]
                        S.add("act", lambda e, src_u=src_u, du=du: e.activation(out=du, in_=src_u, func=AF.Copy),
                              reads=[("ps", bo[h])], writes=[("U2s", pair, r4, h)])
                        S.add("dve", lambda e, src_l=src_l, dl=dl: e.tensor_copy(out=dl, in_=src_l),
                              reads=[("ps", bo[h]), ("U2s", pair, r4, h)], writes=[("L2s", pair, r4, h)])
                calls.append(dict(
                    nblk=4,
                    Kcur=lambda q, h, pair=pair, blk=blk, par=par: K2T[par][hsl(h), pair, blk(q)],
                    Kprev=lambda q, h, pair=pair, blk=blk, par=par: K2T[1 - par][hsl(h), pair, blk(q)],
                    Qblk=lambda q, h, pair=pair, blk=blk: Q2T[hsl(h), pair, blk(q)],
                    Vcur=lambda q, h, pair=pair, r4=r4, par=par: V2A[par][:, r4 * 4 + q, pair * 2 + h, :],
                    Vprev=lambda q, h, pair=pair, r4=r4, par=par: V2A[1 - par][:, r4 * 4 + q, pair * 2 + h, :],
                    kkeys=lambda q, pair=pair, r4=r4, par=par: [("K2T", par, pair, r4), ("K2T", 1 - par, pair, r4),
                                                              ("Q2T", pair, r4)],
                    vkeys=lambda q, r4=r4, par=par: [("V2A", par, (r4 * 4 + q) // 2), ("V2A1", par, r4 * 4 + q),
                                                    ("V2A", 1 - par, (r4 * 4 + q) // 2), ("V2A1", 1 - par, r4 * 4 + q)],
                    evac=evac))
        ngen = norm_gen(s + 1) if (s + 1 < min(nsb, 3)) else None

        def after_call(ci, ngen=ngen):
            if ngen is not None:
                next(ngen, None)
        attention_run(calls, pt_rr, after_call=after_call)
        if ngen is not None:
            for _ in ngen:
                pass
            norm_done.add(s + 1)
        base = (s - 1) * 2048
        S.add("aq", lambda e, base=base: e.dma_start(out=u2_d[:, :, base:base + 2048], in_=U2s),
              reads=[("U2s", p_, r_, h_) for p_ in range(2) for r_ in range(4) for h_ in range(2)],
              writes=[("u2_d", s)], dma_sem=sem_ul, barrier_dma=True)
        S.add("aq", lambda e, base=base: e.dma_start(out=l2_d[:, :, base:base + 2048], in_=L2s),
              reads=[("L2s", p_, r_, h_) for p_ in range(2) for r_ in range(4) for h_ in range(2)],
              writes=[("l2_d", s)], dma_sem=sem_ul, barrier_dma=True)

    S.barrier()
    A.reset()
    xt = [A.alloc([128, 4, D], F32) for i in range(2)]
    hT = A.alloc([128, KC, 512], BF16)
    cv = A.alloc([128, 4, 514], F32)
    tmp = [A.alloc([128, 512], F32) for i in range(6)]
    yaT = A.alloc([128, 4, 512], BF16)
    QT01 = A.alloc([128, 4, 512], BF16)
    K01T = [A.alloc([128, 4, 512], BF16) for i in range(2)]
    V01A = [A.alloc([128, 8, 4, 128], BF16) for i in range(2)]
    U01 = A.alloc([128, 4, 512], BF16)
    U2t = [A.alloc([128, 2, 512], BF16) for i in range(2)]
    L2t = [A.alloc([128, 2, 512], F32) for i in range(2)]
    Zacc = A.alloc([128, 2, 512], F32)
    mT = A.alloc([128, 8, 512], BF16)
    actT = A.alloc([128, 22, 512], BF16)
    WS = [A.alloc([128, KC * 512], BF16) for i in range(NWS)]
    print("phase-2 arena bytes:", A.off, " persistent bytes:", sb_bytes[0] - ARENA_BYTES)
    assert A.hi <= ARENA_BYTES, (A.hi, ARENA_BYTES)
    sem_ws = [newsem("s_ws%d" % i) for i in range(NWS)]
    sem_xt = [newsem("s_xt%d" % i) for i in range(2)]
    sem_ult = [newsem("s_ult%d" % i) for i in range(2)]
    sem_out = newsem("s_out")
    sem_dbg = newsem("s_dbg")

    S.add("pool", lambda e: e.memset(cv, 0.0), writes=[("cv", c) for c in range(4)])

    ws_rr = [0]
    ws_free = [True] * NWS

    def load_slab(src_ap, kc_n, width, ckey):
        for t_ in range(NWS):
            i = (ws_rr[0] + t_) % NWS
            if ws_free[i]:
                break
        else:
            raise AssertionError("weight slab ring overflow")
        ws_rr[0] = i + 1
        ws_free[i] = False
        dst = WS[i][:, 0:kc_n * width].rearrange("p (k c) -> p k c", k=kc_n)
        S.add("sp", lambda e: e.dma_start(out=dst, in_=src_ap.rearrange("(k p) c -> p k c", p=128)),
              writes=[("ws", i)], dma_sem=sem_ws[i])
        return (dst, i)

    def release(slab):
        ws_free[slab[1]] = True

    def xt_load(j):
        par = j % 2
        tok0 = 2048 + 512 * j
        S.add("sp", lambda e: e.dma_start(out=xt[par], in_=x_loc[tok0:tok0 + 512, :].rearrange(
            "(s p) d -> p s d", p=128)), writes=[("xt", par, s_) for s_ in range(4)], dma_sem=sem_xt[par])

    def ul_load(j):
        par = j % 2
        n = 512 if j < 8 else 128
        o1 = S.add("sp", lambda e: e.dma_start(out=U2t[par][:, :, 0:n], in_=u2_d[:, :, 512 * j:512 * j + n]),
                   reads=[("u2_d", s_) for s_ in range(1, nsb)], writes=[("U2t", par)], dma_sem=sem_ult[par])
        o2 = S.add("sp", lambda e: e.dma_start(out=L2t[par][:, :, 0:n], in_=l2_d[:, :, 512 * j:512 * j + n]),
                   reads=[("l2_d", s_) for s_ in range(1, nsb)], writes=[("L2t", par)], dma_sem=sem_ult[par])
        o1.ev = o2.ev

    def fm_group(slab, kc_n, col0, rhs_fn, rkeys, N=512):
        b = next_ps()
        for kc in range(kc_n):
            S.add("pe", lambda e, b=b, kc=kc: e.matmul(
                ps[b][:, 0:N], lhsT=slab[0][:, kc, col0:col0 + 128], rhs=rhs_fn(kc),
                start=(kc == 0), stop=(kc == kc_n - 1)),
                reads=[("ws", slab[1])] + rkeys, writes=[("ps", b)])
        return b

    hT_keys = [("hT", s_) for s_ in range(4)]
    tmp_rr = [0]

    def next_tmp():
        i = tmp_rr[0] % 6
        tmp_rr[0] += 1
        return i

    pending_stores = []

    def flush_stores():
        for f in pending_stores:
            f()
        del pending_stores[:]

    ntile = dbg.get("_ntiles", NT2)

    def n1(j):
        par = j % 2
        xsrc = xt[par]
        xk = lambda s_: ("xt", par, s_)
        TW = 512 if j < 8 else 128
        NS = TW // 128
        NS = 4 if j < 8 else 1
        norm_stats(lambda s_: xsrc[:, s_, :], NS, 0, xk)
        norm_transpose(lambda s_: xsrc[:, s_, :], NS, g1b, "g1b", 0, xk,
                       lambda s_: hT[:, :, s_ * 128:(s_ + 1) * 128], lambda v: v, lambda s_: [("hT", s_)])

    def tile(j, mid_hook=None):
        par = j % 2
        pre = (j < 0)
        xsrc = xt[par]
        xk = lambda s_: ("xt", par, s_)
        TW = 512 if j < 8 else 128
        NS = TW // 128
        for (c0, dst, dk, need) in ((2304, K01T[par], ("K01T", par), True), (1536, QT01, ("QT01",), not pre)):
            if not need:
                continue
            sl = load_slab(win_b[:, c0:c0 + 512], KC, 512, "win")
            for c in range(4):
                b = fm_group(sl, KC, c * 128, lambda kc: hT[:, kc, 0:TW], hT_keys, TW)
                if c < 2:
                    S.add("act", lambda e, b=b, dst=dst, c=c: e.activation(out=dst[:, c, 0:TW], in_=ps[b][:, 0:TW], func=AF.Copy),
                          reads=[("ps", b)], writes=[dk + (c,)])
                else:
                    S.add("act", lambda e, b=b, dst=dst, c=c: e.activation(
                        out=dst[:, c, :].rearrange("p (r i) -> p r i", r=4)[:, :, 0:TW // 4],
                        in_=ps[b][:, 0:TW].rearrange("p (i r) -> p r i", r=4), func=AF.Copy),
                        reads=[("ps", b)], writes=[dk + (c,)])
            release(sl)
        slV = load_slab(win_b[:, 3072:3584], KC, 512, "win")
        for g in range(2):
            for k2 in range(2):
                if TW < 512 and g == 0 and k2 == 1:
                    continue
                b = next_ps()
                nk = 1 if (TW < 512 and g == 0) else 2
                MV = 128 if (TW == 512 or g == 0) else TW // 4
                for kk in range(nk):
                    kb = k2 * 2 + kk
                    for kc in range(KC):
                        if g == 0:
                            lhs = hT[:, kc, kb * 128:(kb + 1) * 128]
                        else:
                            lhs = hT[:, kc, 0:TW].rearrange("p (i r) -> p r i", r=4)[:, kb, :]
                        S.add("pe", lambda e, b=b, kk=kk, kc=kc, lhs=lhs, g=g, MV=MV: e.matmul(
                            ps[b][0:MV, kk * 256:(kk + 1) * 256], lhsT=lhs, rhs=slV[0][:, kc, g * 256:(g + 1) * 256],
                            start=(kc == 0), stop=(kc == KC - 1), skip_group_check=True),
                            reads=[("ws", slV[1])] + hT_keys, writes=[("ps", b)])
                S.add("act", lambda e, b=b, g=g, k2=k2, nk=nk, MV=MV: e.activation(
                    out=V01A[par][0:MV, g * 4 + k2 * 2: g * 4 + k2 * 2 + nk, :, 0:64],
                    in_=ps[b][0:MV, 0:nk * 256].rearrange("p (r h e) -> p r h e", r=nk, h=4), func=AF.Copy),
                    reads=[("ps", b)], writes=[("V01A", par, g, k2)])
                for kk in range(2):
                    kb = k2 * 2 + kk
                    vt = vld0 if g == 0 else vld1
                    col = (j + 1) * 4 + kb
                    S.add("dve", lambda e, g=g, kb=kb, vt=vt, col=col: e.tensor_scalar(
                        out=V01A[par][:, g * 4 + kb, :, 64:128], in0=ones3, scalar1=vt[:, col:col + 1],
                        scalar2=None, op0=ALU.mult),
                        reads=["ones3", "vld0", "vld1"], writes=[("V01A1", par, g, kb)])
        release(slV)
        flush_stores()
        if pre:
            return
        slC = load_slab(win_b[:, 512:1024], KC, 512, "win")
        slVv = load_slab(win_b[:, 1024:1536], KC, 512, "win")
        slB = load_slab(win_b[:, 0:512], KC, 512, "win")
        for c in range(4):
            b1 = fm_group(slC, KC, c * 128, lambda kc: hT[:, kc, 0:TW], hT_keys, TW)
            t1 = next_tmp()
            S.add("act", lambda e, b1=b1, t1=t1: e.activation(out=tmp[t1][:, 0:TW], in_=ps[b1][:, 0:TW], func=AF.Copy),
                  reads=[("ps", b1)], writes=[("tmp", t1)])
            b2 = fm_group(slVv, KC, c * 128, lambda kc: hT[:, kc, 0:TW], hT_keys, TW)
            S.add("dve", lambda e, b2=b2, t1=t1, c=c: e.tensor_tensor(out=cv[:, c, 2:2 + TW], in0=ps[b2][:, 0:TW],
                                                                     in1=tmp[t1][:, 0:TW], op=ALU.mult),
                  reads=[("ps", b2), ("tmp", t1)], writes=[("cv", c)])
            t2 = next_tmp()
            S.add("dve", lambda e, t2=t2, c=c: e.tensor_scalar(
                out=tmp[t2][:, 0:TW], in0=cv[:, c, 2:2 + TW], scalar1=caw[:, c * 3 + 2:c * 3 + 3], scalar2=cab[:, c:c + 1],
                op0=ALU.mult, op1=ALU.add), reads=[("cv", c), "caw", "cab"], writes=[("tmp", t2)])
            S.add("dve", lambda e, t2=t2, c=c: e.scalar_tensor_tensor(
                out=tmp[t2][:, 0:TW], in0=cv[:, c, 1:1 + TW], scalar=caw[:, c * 3 + 1:c * 3 + 2], in1=tmp[t2][:, 0:TW],
                op0=ALU.mult, op1=ALU.add), reads=[("cv", c), "caw", ("tmp", t2)], writes=[("tmp", t2)])
            S.add("dve", lambda e, t2=t2, c=c: e.scalar_tensor_tensor(
                out=tmp[t2][:, 0:TW], in0=cv[:, c, 0:TW], scalar=caw[:, c * 3:c * 3 + 1], in1=tmp[t2][:, 0:TW],
                op0=ALU.mult, op1=ALU.add), reads=[("cv", c), "caw", ("tmp", t2)], writes=[("tmp", t2)])
            S.add("pool", lambda e, c=c: e.tensor_copy(out=cv[:, c, 0:2], in_=cv[:, c, TW:TW + 2]),
                  reads=[("cv", c)], writes=[("cv", c)])
            b3 = fm_group(slB, KC, c * 128, lambda kc: hT[:, kc, 0:TW], hT_keys, TW)
            S.add("dve", lambda e, b3=b3, t2=t2, c=c: e.tensor_tensor(out=yaT[:, c, 0:TW], in0=ps[b3][:, 0:TW],
                                                                     in1=tmp[t2][:, 0:TW], op=ALU.mult),
                  reads=[("ps", b3), ("tmp", t2)], writes=[("yaT", c)])
        release(slC)
        release(slVv)
        release(slB)
        hsl = lambda h: slice(h * 64, (h + 1) * 64)
        blk = lambda q: slice(q * 128, (q + 1) * 128)
        calls = []
        for g in range(2):
            for pair in range(2):
                c = g * 2 + pair
                if g == 0:
                    Kprev = lambda q, h, c=c: (K01T[par][hsl(h), c, blk(q - 1)] if q > 0
                                               else K01T[1 - par][hsl(h), c, blk(3)])
                    Vprev = lambda q, h, pair=pair: (V01A[par][:, q - 1, pair * 2 + h, :] if q > 0
                                                     else V01A[1 - par][:, 3, pair * 2 + h, :])
                    vkeys = lambda q: ([("V01A", par, 0, q // 2), ("V01A1", par, 0, q)] +
                                       ([("V01A", par, 0, (q - 1) // 2), ("V01A1", par, 0, q - 1)] if q > 0 else
                                        [("V01A", 1 - par, 0, 1), ("V01A1", 1 - par, 0, 3)]))
                else:
                    Kprev = lambda q, h, c=c: K01T[1 - par][hsl(h), c, blk(q)]
                    Vprev = lambda q, h, pair=pair: V01A[1 - par][:, 4 + q, pair * 2 + h, :]
                    vkeys = lambda q: [("V01A", par, 1, q // 2), ("V01A1", par, 1, q),
                                       ("V01A", 1 - par, 1, q // 2), ("V01A1", 1 - par, 1, q)]

                def evac(bo, g=g, pair=pair, c=c):
                    for h in range(2):
                        hs = hsl(h)
                        if g == 0:
                            su, sl_ = ps[bo[h]][0:64, 0:TW], ps[bo[h]][64:128, 0:TW]
                            du = U01[hs, c, 0:TW]
                            dz = Zacc[hs, pair, 0:TW]
                        else:
                            su = ps[bo[h]][0:64, :].rearrange("p (r i) -> p i r", r=4)[:, 0:TW // 4, :]
                            sl_ = ps[bo[h]][64:128, :].rearrange("p (r i) -> p i r", r=4)[:, 0:TW // 4, :]
                            du = U01[hs, c, 0:TW].rearrange("p (i r) -> p i r", r=4)
                            dz = Zacc[hs, pair, 0:TW].rearrange("p (i r) -> p i r", r=4)
                        S.add("act", lambda e, su=su, du=du: e.activation(out=du, in_=su, func=AF.Copy),
                              reads=[("ps", bo[h])], writes=[("U01", c, h)])
                        if g == 0:
                            n = 512 if j < 8 else 128
                            S.add("dve", lambda e, sl_=sl_, dz=dz: e.tensor_copy(out=dz, in_=sl_),
                                  reads=[("ps", bo[h]), ("U01", c, h)], writes=[("Zacc", pair, h)])
                            S.add("dve", lambda e, hs=hs, pair=pair, n=n: e.tensor_tensor(
                                out=Zacc[hs, pair, 0:n], in0=Zacc[hs, pair, 0:n], in1=L2t[par][hs, pair, 0:n],
                                op=ALU.add), reads=[("Zacc", pair, h), ("L2t", par)], writes=[("Zacc", pair, h)])
                        else:
                            S.add("dve", lambda e, sl_=sl_, dz=dz: e.tensor_tensor(out=dz, in0=sl_, in1=dz, op=ALU.add),
                                  reads=[("ps", bo[h]), ("Zacc", pair, h), ("U01", c, h)], writes=[("Zacc", pair, h)])
                calls.append(dict(
                    nblk=(4 if (g == 1 or TW == 512) else 1),
                    Kcur=lambda q, h, c=c: K01T[par][hsl(h), c, blk(q)],
                    Kprev=Kprev,
                    Qblk=lambda q, h, c=c: QT01[hsl(h), c, blk(q)],
                    Vcur=lambda q, h, pair=pair, g=g: V01A[par][:, g * 4 + q, pair * 2 + h, :],
                    Vprev=Vprev,
                    kkeys=lambda q, c=c: [("K01T", par, c), ("K01T", 1 - par, c), ("QT01", c)],
                    vkeys=vkeys, evac=evac))
        attention_run(calls, pt_rr)
        zk = [("Zacc", p_, h) for p_ in range(2) for h in range(2)]
        S.add("dve", lambda e: e.tensor_scalar(out=Zacc[:, :, 0:TW], in0=Zacc[:, :, 0:TW], scalar1=1e-30, scalar2=None,
                                               op0=ALU.add), reads=zk, writes=zk)
        S.add("dve", lambda e: e.reciprocal(out=Zacc[:, :, 0:TW], in_=Zacc[:, :, 0:TW]), reads=zk, writes=zk)
        n2 = 512 if j < 8 else 128
        for c in range(6):
            if c < 4:
                src, rk, n = U01[:, c, 0:TW], [("U01", c, 0), ("U01", c, 1)], TW
            else:
                src, rk, n = U2t[par][:, c - 4, 0:n2], [("U2t", par)], n2
            eng = "pool" if c % 2 == 0 else "dve"
            S.add(eng, lambda e, c=c, src=src, n=n: e.tensor_tensor(out=src, in0=src, in1=Zacc[:, c % 2, 0:n],
                                                                  op=ALU.mult),
                  reads=rk + zk, writes=[("yb", c)] + rk)
        yb = lambda kc: (U01[:, kc, 0:TW] if kc < 4 else U2t[par][:, kc - 4, 0:TW])
        ybk = [("yb", c) for c in range(6)] + [("U01", c, h) for c in range(4) for h in range(2)] + [("U2t", par)]
        if j + 1 < ntile:
            xt_load(j + 1)
            ul_load(j + 1)
        slPA = load_slab(wpa_b[:, :], 4, 1024, "wpa")
        yak = [("yaT", c) for c in range(4)]
        for hf in range(2):
            slGA = load_slab(win_b[:, 3840 + hf * 512: 3840 + (hf + 1) * 512], KC, 512, "win")
            for o4 in range(4):
                oc = hf * 4 + o4
                bga = fm_group(slGA, KC, o4 * 128, lambda kc: hT[:, kc, 0:TW], hT_keys, TW)
                ta = next_tmp()
                S.add("act", lambda e, bga=bga, ta=ta, oc=oc: e.activation(out=tmp[ta][:, 0:TW], in_=ps[bga][:, 0:TW],
                                                                          func=AF.Sigmoid, bias=bg[:, oc:oc + 1]),
                      reads=[("ps", bga), "bg"], writes=[("tmp", ta)])
                bpa = fm_group(slPA, 4, oc * 128, lambda kc: yaT[:, kc, 0:TW], yak, TW)
                S.add("dve", lambda e, bpa=bpa, ta=ta: e.tensor_tensor(out=tmp[ta][:, 0:TW], in0=ps[bpa][:, 0:TW],
                                                                      in1=tmp[ta][:, 0:TW], op=ALU.mult),
                      reads=[("ps", bpa), ("tmp", ta)], writes=[("tmp", ta)])
                S.add("pool", lambda e, ta=ta, oc=oc: e.tensor_copy(out=mT[:, oc, 0:TW], in_=tmp[ta][:, 0:TW]),
                      reads=[("tmp", ta)], writes=[("mT", oc)])
            release(slGA)
            slGB = load_slab(win_b[:, 4864 + hf * 512: 4864 + (hf + 1) * 512], KC, 512, "win")
            slPB = load_slab(wpb_b[:, hf * 512:(hf + 1) * 512], 6, 512, "wpb")
            for o4 in range(4):
                oc = hf * 4 + o4
                bgb = fm_group(slGB, KC, o4 * 128, lambda kc: hT[:, kc, 0:TW], hT_keys, TW)
                tb = next_tmp()
                S.add("act", lambda e, bgb=bgb, tb=tb, oc=oc: e.activation(out=tmp[tb][:, 0:TW], in_=ps[bgb][:, 0:TW],
                                                                          func=AF.Sigmoid, bias=bg[:, 8 + oc:8 + oc + 1]),
                      reads=[("ps", bgb), "bg"], writes=[("tmp", tb)])
                bpb = fm_group(slPB, 6, o4 * 128, yb, ybk, TW)
                S.add("dve", lambda e, bpb=bpb, tb=tb: e.tensor_tensor(out=tmp[tb][:, 0:TW], in0=ps[bpb][:, 0:TW],
                                                                      in1=tmp[tb][:, 0:TW], op=ALU.mult),
                      reads=[("ps", bpb), ("tmp", tb)], writes=[("tmp", tb)])
                S.add("pool", lambda e, tb=tb, oc=oc: e.tensor_tensor(out=mT[:, oc, 0:TW], in0=mT[:, oc, 0:TW],
                                                                     in1=tmp[tb][:, 0:TW], op=ALU.add),
                      reads=[("tmp", tb), ("mT", oc)], writes=[("mT", oc)])
            release(slGB)
            release(slPB)
        release(slPA)
        mk = [("mT", oc) for oc in range(8)]
        slo = [load_slab(wout_b[:, hf * 512:(hf + 1) * 512], KC, 512, "wout") for hf in range(2)]
        for s_ in range(NS):
            for hf in range(2):
                sl = slo[hf]
                b = next_ps()
                for kc in range(KC):
                    S.add("pe", lambda e, b=b, kc=kc, s_=s_, sl=sl: e.matmul(
                        ps[b], lhsT=mT[:, kc, s_ * 128:(s_ + 1) * 128], rhs=sl[0][:, kc, :],
                        start=(kc == 0), stop=(kc == KC - 1)),
                        reads=[("ws", sl[1])] + mk, writes=[("ps", b)])
                S.add("dve", lambda e, b=b, s_=s_, hf=hf: e.tensor_tensor(
                    out=xsrc[:, s_, hf * 512:(hf + 1) * 512], in0=ps[b], in1=xsrc[:, s_, hf * 512:(hf + 1) * 512],
                    op=ALU.add), reads=[("ps", b), xk(s_)], writes=[xk(s_)])
        release(slo[0])
        release(slo[1])
        if "x1" in dbg_outs and j == dbg.get("_tile", 0):
            S.add("sp", lambda e: e.dma_start(out=dbg_outs["x1"].rearrange("(s p) d -> p s d", p=128), in_=xsrc),
                  reads=[xk(s_) for s_ in range(4)], dma_sem=sem_dbg)
            for nm, ap_, rk_ in (("ya", yaT, yak), ("yb", U01, ybk), ("y2", U2t[par], ybk), ("mT", mT, mk),
                                 ("z", Zacc, zk)):
                if nm in dbg_outs:
                    S.add("sp", lambda e, nm=nm, ap_=ap_: e.dma_start(out=dbg_outs[nm], in_=ap_),
                          reads=rk_, dma_sem=sem_dbg)
        for s1 in range(NS):
            norm_stats(lambda s_, s1=s1: xsrc[:, s1, :], 1, 4 + s1, lambda s_, s1=s1: xk(s1))
        norm_transpose(lambda s_: xsrc[:, s_, :], NS, g2b, "g2b", 4, xk,
                       lambda s_: hT[:, :, s_ * 128:(s_ + 1) * 128], lambda v: v, lambda s_: [("hT", s_)])
        silu_q = []

        def emit_silu():
            tc_, ch = silu_q.pop(0)
            S.add("act", lambda e, tc_=tc_, ch=ch: e.activation(out=actT[:, ch, 0:TW], in_=tmp[tc_][:, 0:TW], func=AF.Silu),
                  reads=[("tmp", tc_)], writes=[("actT", ch)])

        for si in range(11):
            sl = load_slab(wup_b[:, si * 512:(si + 1) * 512], KC, 512, "wup")
            for c4 in range(4):
                ch = si * 4 + c4
                b = fm_group(sl, KC, c4 * 128, lambda kc: hT[:, kc, 0:TW], hT_keys, TW)
                tc_ = next_tmp()
                w2_, w1_, w0_ = (fcw[:, ch * 3 + 2:ch * 3 + 3], fcw[:, ch * 3 + 1:ch * 3 + 2], fcw[:, ch * 3:ch * 3 + 1])
                S.add("act", lambda e, b=b, ch=ch: e.activation(out=uc[:, par, ch, :], in_=ps[b][:, TW - 2:TW], func=AF.Copy),
                      reads=[("ps", b)], writes=[("uc", par, ch)])
                S.add("act", lambda e, tc_=tc_, b=b, ch=ch, w2_=w2_: e.activation(
                    out=tmp[tc_][:, 0:TW], in_=ps[b][:, 0:TW], func=AF.Identity, scale=w2_, bias=fcb[:, ch:ch + 1]),
                    reads=[("ps", b), "fcw", "fcb", ("uc", par, ch)], writes=[("tmp", tc_)])
                S.add("dve", lambda e, tc_=tc_, b=b, w1_=w1_: e.scalar_tensor_tensor(
                    out=tmp[tc_][:, 1:TW], in0=ps[b][:, 0:TW - 1], scalar=w1_, in1=tmp[tc_][:, 1:TW],
                    op0=ALU.mult, op1=ALU.add), reads=[("ps", b), ("tmp", tc_), "fcw"], writes=[("tmp", tc_)])
                S.add("dve", lambda e, tc_=tc_, b=b, w0_=w0_: e.scalar_tensor_tensor(
                    out=tmp[tc_][:, 2:TW], in0=ps[b][:, 0:TW - 2], scalar=w0_, in1=tmp[tc_][:, 2:TW],
                    op0=ALU.mult, op1=ALU.add), reads=[("ps", b), ("tmp", tc_), "fcw"], writes=[("tmp", tc_)])
                S.add("dve", lambda e, tc_=tc_, ch=ch, w1_=w1_: e.scalar_tensor_tensor(
                    out=tmp[tc_][:, 0:1], in0=uc[:, 1 - par, ch, 1:2], scalar=w1_, in1=tmp[tc_][:, 0:1],
                    op0=ALU.mult, op1=ALU.add), reads=[("uc", 1 - par, ch), ("tmp", tc_), "fcw"], writes=[("tmp", tc_)])
                S.add("dve", lambda e, tc_=tc_, ch=ch, w0_=w0_: e.scalar_tensor_tensor(
                    out=tmp[tc_][:, 0:2], in0=uc[:, 1 - par, ch, 0:2], scalar=w0_, in1=tmp[tc_][:, 0:2],
                    op0=ALU.mult, op1=ALU.add), reads=[("uc", 1 - par, ch), ("tmp", tc_), "fcw"], writes=[("tmp", tc_)])
                if ch < 22:
                    silu_q.append((tc_, ch))
                    if len(silu_q) > 2:
                        emit_silu()
                else:
                    while silu_q:
                        emit_silu()
                    S.add("pool", lambda e, tc_=tc_, ch=ch: e.tensor_tensor(
                        out=actT[:, ch - 22, 0:TW], in0=actT[:, ch - 22, 0:TW], in1=tmp[tc_][:, 0:TW], op=ALU.mult),
                        reads=[("tmp", tc_), ("actT", ch - 22)], writes=[("actT", ch - 22)])
            release(sl)
        ak = [("actT", c) for c in range(22)]
        for hf in range(2):
            banks = [next_ps() for _ in range(4)]
            for k0 in (0, 8, 16):
                kn = min(8, 22 - k0)
                sl = load_slab(wdn_b[k0 * 128:(k0 + kn) * 128, hf * 512:(hf + 1) * 512], kn, 512, "wdn")
                for s_ in range(NS):
                    for kk in range(kn):
                        kc = k0 + kk
                        S.add("pe", lambda e, b=banks[s_], kc=kc, kk=kk, s_=s_, sl=sl: e.matmul(
                            ps[b], lhsT=actT[:, kc, s_ * 128:(s_ + 1) * 128], rhs=sl[0][:, kk, :],
                            start=(kc == 0), stop=(kc == 21), skip_group_check=True),
                            reads=[("ws", sl[1]), ("actT", kc)], writes=[("ps", banks[s_])])
                release(sl)
            for s_ in range(NS):
                S.add("dve", lambda e, b=banks[s_], s_=s_, hf=hf: e.tensor_tensor(
                    out=xsrc[:, s_, hf * 512:(hf + 1) * 512], in0=ps[b], in1=xsrc[:, s_, hf * 512:(hf + 1) * 512],
                    op=ALU.add), reads=[("ps", banks[s_]), xk(s_)], writes=[xk(s_)])
            if hf == 0 and mid_hook is not None:
                mid_hook()
        norm_stats(lambda s_: xsrc[:, s_, :], NS, 8, xk)
        for s_ in range(4):
            row = 512 * j + 128 * s_ - 128
            if row < 0 or row >= NOUT:
                continue
            S.add("dve", lambda e, s_=s_: e.scalar_tensor_tensor(
                out=xsrc[:, s_, :], in0=xsrc[:, s_, :], scalar=rstd[:, 8 + s_:9 + s_], in1=g3b,
                op0=ALU.mult, op1=ALU.mult), reads=[xk(s_), ("rstd", 8 + s_), "g3b"], writes=[xk(s_)])
            pending_stores.append(lambda s_=s_, row=row: S.add(
                "sp", lambda e: e.dma_start(out=out_d[row:row + 128, :], in_=xsrc[:, s_, :]),
                reads=[xk(s_)], dma_sem=sem_out))

    S.add("sp", lambda e: e.dma_start(out=xt[1], in_=x_loc[1536:2048, :].rearrange("(s p) d -> p s d", p=128)),
          writes=[("xt", 1, s_) for s_ in range(4)], dma_sem=sem_xt[1])
    xt_load(0)
    ul_load(0)
    n1(-1)
    tile(-1)
    n1(0)
    for j in range(0, ntile):
        tile(j, mid_hook=(lambda j=j: n1(j + 1)) if j + 1 < ntile else None)
    flush_stores()

    final_wait = []
    for sem in (sem_out, sem_dbg, sem_ul):
        if S.dma_count.get(sem, 0):
            final_wait.append((sem, S.dma_count[sem]))

    eng_sems = {e: [] for e in COMPUTE}
    for e in COMPUTE:
        nsig = sum(1 for op in S.ops[e] if op.signal and op.dma_sem is None)
        for i in range(nsig // SEM_LIMIT + 1):
            eng_sems[e].append(newsem("s_%s%d" % (e, i)))
        c = 0
        for op in S.ops[e]:
            if op.dma_sem is None and op.signal:
                op.ev = (eng_sems[e][c // SEM_LIMIT], c % SEM_LIMIT + 1)
                c += 1
        print("engine", e, "ops", len(S.ops[e]), "signals", nsig)
    print("sp ops", len(S.ops["sp"]))

    def lower(ename, tail=None):
        acts, waited = [], {}
        for op in S.ops[ename]:
            for d in op.deps:
                sem, val = d.ev
                if waited.get(id(sem), 0) < val:
                    acts.append(("wait", sem, val, op))
                    waited[id(sem)] = val
            if op.dma_sem is not None:
                acts.append(("op", op.dma_sem, 16, op))
            elif op.signal:
                acts.append(("op", op.ev[0], 1, op))
            else:
                acts.append(("op", None, 0, op))
        for sem, val in (tail or []):
            acts.append(("wait", sem, val, None))
        return acts

    streams = {e: lower(e, final_wait if e == "sp" else None) for e in S.ops}
    semv = {}
    pos = {e: 0 for e in streams}
    progress = True
    while progress:
        progress = False
        for e, acts in streams.items():
            while pos[e] < len(acts):
                a = acts[pos[e]]
                if a[0] == "wait":
                    if semv.get(id(a[1]), 0) < a[2]:
                        break
                elif a[1] is not None:
                    semv[id(a[1])] = semv.get(id(a[1]), 0) + a[2]
                pos[e] += 1
                progress = True
    for e, acts in streams.items():
        if pos[e] < len(acts):
            a = acts[pos[e]]
            own = [q for q, ss in eng_sems.items() if any(x is a[1] for x in ss)]
            print("DEADLOCK: queue", e, "stuck at", pos[e], "/", len(acts), a[0], "need", a[2], "have", semv.get(id(a[1]), 0),
                  "op idx", getattr(a[3], "idx", None), "sem of", own, "deps", [(d.eng, d.idx, d.ev[1]) for d in (a[3].deps if a[3] else [])])
    if any(pos[e] < len(streams[e]) for e in streams):
        raise RuntimeError("semaphore protocol deadlock")
    print("protocol check ok")

    block = es.enter_context(nc.Block())

    def emit_stream(ename, handle, tail=None):
        waited = {}
        for op in S.ops[ename]:
            for d in op.deps:
                sem, val = d.ev
                if waited.get(id(sem), 0) < val:
                    handle.wait_ge(sem, val)
                    waited[id(sem)] = val
            ins = op.fn(handle)
            if op.dma_sem is not None:
                ins.then_inc(op.dma_sem, 16)
            elif op.signal:
                ins.then_inc(op.ev[0], 1)
        if tail:
            for sem, val in tail:
                handle.wait_ge(sem, val)

    @block.tensor
    def _(pe):
        emit_stream("pe", pe)

    @block.scalar
    def _(act):
        emit_stream("act", act)

    @block.vector
    def _(dve):
        emit_stream("dve", dve)

    @block.gpsimd
    def _(pool):
        emit_stream("pool", pool)

    @block.sync
    def _(sp):
        emit_stream("sp", sp, tail=final_wait)

    es.close()
    return nc


def _prep_core(c, x, consts):
    b, hf = c // 2, c % 2
    t0 = hf * 4096
    lo = t0 - OUT0
    xl = np.zeros((NLOC, D), np.float32)
    s0, s1 = max(lo, 0), min(lo + NLOC, 8192)
    xl[s0 - lo:s1 - lo] = x[b, s0:s1]
    tok_valid = np.ones(NLOC + 2048, np.float32)
    if hf == 0:
        tok_valid[:OUT0] = 0.0
    i = np.arange(128)
    vld0 = np.zeros((128, 40), np.float32)
    vld1 = np.zeros((128, 40), np.float32)
    for j in range(-1, 9):
        for k in range(4):
            col = (j + 1) * 4 + k
            vld0[:, col] = tok_valid[2048 + 512 * j + 128 * k + i]
            vld1[:, col] = tok_valid[2048 + 512 * j + 4 * i + k]
    vld2 = np.zeros((128, 64), np.float32)
    for s in range(4):
        for r in range(16):
            vld2[:, s * 16 + r] = tok_valid[2048 * s + 16 * i + r]
    perm = np.zeros((128, 128), np.float32)
    t = np.arange(128)
    perm[t, (t % 16) * 8 + t // 16] = 1.0
    m = dict(consts)
    m.update(x_loc=xl, vld0=vld0, vld1=vld1, vld2=vld2, perm16=perm)
    return m


_NC_CACHE = {}


def kernel(x, norm_mix_g, w_in, b_gate, conv_a_w, conv_a_b, w_proj_a, w_proj_b, w_out, norm_ffn_g,
           w_up, ffn_conv_w, ffn_conv_b, w_down, final_norm_g, _dbg=None):
    f = lambda a: np.ascontiguousarray(np.asarray(a, dtype=np.float32))
    x = f(x)
    consts = dict(
        w_in=f(w_in)[0], w_pa=f(w_proj_a)[0], w_pb=f(w_proj_b)[0], w_out=f(w_out)[0], w_up=f(w_up)[0],
        w_dn=f(w_down)[0],
        g1=f(norm_mix_g).reshape(1, D), g2=f(norm_ffn_g).reshape(1, D), g3=f(final_norm_g).reshape(1, D),
        bg=f(f(b_gate)[0].reshape(2, 8, 128).transpose(2, 0, 1).reshape(128, 16)),
        caw=f(f(conv_a_w)[0].reshape(3, 4, 128).transpose(2, 1, 0).reshape(128, 12)),
        cab=f(f(conv_a_b)[0].reshape(4, 128).T),
        fcw=f(f(ffn_conv_w)[0].reshape(3, 44, 128).transpose(2, 1, 0).reshape(128, 132)),
        fcb=f(f(ffn_conv_b)[0].reshape(44, 128).T),
    )
    key = repr(sorted((_dbg or {}).items()))
    if key not in _NC_CACHE:
        _NC_CACHE[key] = build_program(_dbg)
    nc = _NC_CACHE[key]
    in_maps = [_prep_core(c, x, consts) for c in range(8)]
    res = run_bass_kernel_spmd(nc, in_maps, core_ids=list(range(8)))
    out = np.empty((4, 8192, D), np.float32)
    for c in range(8):
        out[c // 2, (c % 2) * 4096:(c % 2 + 1) * 4096] = res.results[c]["out"]
    if _dbg:
        return out, res
    return out
```

```python
import numpy as np
from contextlib import ExitStack
import concourse.bass as bass
import concourse.mybir as mybir
from concourse.bass_utils import run_bass_kernel_spmd

F32 = mybir.dt.float32
BF16 = mybir.dt.bfloat16
AF = mybir.ActivationFunctionType
ALU = mybir.AluOpType

D = 1024
KC = 8
DIN = 5888
DFF = 2816
NLOC = 8192
OUT0 = 2176
NOUT = 4096
NT2 = 9
NQ2 = 4224
EPS = 1e-6
SEM_LIMIT = 24000
NWS = 4

COMPUTE = ("pe", "act", "dve", "pool")


class Op:
    __slots__ = ("eng", "fn", "deps", "signal", "ev", "dma_sem", "idx")

    def __init__(self, eng, fn, dma_sem=None):
        self.eng = eng
        self.fn = fn
        self.deps = []
        self.signal = False
        self.ev = None
        self.dma_sem = dma_sem


class Sched:
    def __init__(self):
        self.ops = {e: [] for e in ("pe", "act", "dve", "pool", "sp")}
        self.last_w = {}
        self.readers = {}
        self.dma_count = {}
        self.n = 0
        self.pending_barrier = {}
        self.barrier_dmas = []

    def barrier(self):
        lasts = [q[-1] for q in self.ops.values() if q]
        lasts += self.barrier_dmas
        self.barrier_dmas = []
        self.pending_barrier = {q: list(lasts) for q in self.ops}

    def add(self, eng, fn, reads=(), writes=(), dma_sem=None, barrier_dma=False):
        q = {"gq": "pool", "aq": "act"}.get(eng, eng)
        op = Op(eng, fn, dma_sem)
        op.idx = self.n
        self.n += 1
        isdma = dma_sem is not None
        deps = {}
        for r in reads:
            w = self.last_w.get(r)
            if w is not None:
                deps[id(w)] = w
        for w_ in writes:
            lw = self.last_w.get(w_)
            if lw is not None:
                deps[id(lw)] = lw
            for rd in self.readers.get(w_, {}).values():
                deps[id(rd)] = rd
        if q in self.pending_barrier:
            for d in self.pending_barrier.pop(q):
                deps[id(d)] = d
        for d in deps.values():
            if (not isdma) and d.dma_sem is None and d.eng == eng == "pe":
                continue
            op.deps.append(d)
            d.signal = True
        for r in reads:
            rd = self.readers.setdefault(r, {})
            rd[("dma", op.idx) if isdma else eng] = op
        for w_ in writes:
            self.last_w[w_] = op
            self.readers[w_] = {}
        if isdma:
            c = self.dma_count.get(dma_sem, 0) + 16
            self.dma_count[dma_sem] = c
            op.ev = (dma_sem, c)
            if barrier_dma:
                self.barrier_dmas.append(op)
        self.ops[q].append(op)
        return op


class Arena:
    def __init__(self, ap):
        self.ap = ap
        self.off = 0
        self.hi = 0

    def reset(self):
        self.hi = max(self.hi, self.off)
        self.off = 0

    def alloc(self, shape, dt):
        n = 1
        for s_ in shape[1:]:
            n *= s_
        nb = n * (4 if dt == F32 else 2)
        nb = (nb + 63) // 64 * 64
        v = self.ap[:, self.off // 2:(self.off + nb) // 2]
        self.off += nb
        self.hi = max(self.hi, self.off)
        if dt == F32:
            v = v.bitcast(F32)
        v = v[:, 0:n]
        if len(shape) == 3:
            v = v.rearrange("p (a b) -> p a b", a=shape[1])
        elif len(shape) == 4:
            v = v.rearrange("p (a b c) -> p a b c", a=shape[1], b=shape[2])
        return v


ARENA_BYTES = 176 * 1024


def build_program(dbg=None):
    nc = bass.Bass("TRN2", target_bir_lowering=False)
    S = Sched()
    dbg = dbg or {}

    def din(name, shape, dt=F32):
        return nc.dram_tensor(name, list(shape), dt, kind="ExternalInput").ap()

    x_loc = din("x_loc", [NLOC, D])
    w_in = din("w_in", [D, DIN])
    w_pa = din("w_pa", [512, D])
    w_pb = din("w_pb", [768, D])
    w_out = din("w_out", [D, D])
    w_up = din("w_up", [D, 2 * DFF])
    w_dn = din("w_dn", [DFF, D])
    g1_d = din("g1", [1, D])
    g2_d = din("g2", [1, D])
    g3_d = din("g3", [1, D])
    bg_d = din("bg", [128, 16])
    caw_d = din("caw", [128, 12])
    cab_d = din("cab", [128, 4])
    fcw_d = din("fcw", [128, 132])
    fcb_d = din("fcb", [128, 44])
    vld0_d = din("vld0", [128, 40])
    vld1_d = din("vld1", [128, 40])
    vld2_d = din("vld2", [128, 64])
    perm_d = din("perm16", [128, 128])
    out_d = nc.dram_tensor("out", [NOUT, D], F32, kind="ExternalOutput").ap()

    def dscr(name, shape, dt=BF16):
        return nc.dram_tensor(name, list(shape), dt, kind="Internal").ap()

    win_b = dscr("win_b", [D, DIN])
    wpa_b = dscr("wpa_b", [512, D])
    wpb_b = dscr("wpb_b", [768, D])
    wout_b = dscr("wout_b", [D, D])
    wup_b = dscr("wup_b", [D, 2 * DFF])
    wdn_b = dscr("wdn_b", [DFF, D])
    u2_d = dscr("u2_d", [128, 2, 6144])
    l2_d = dscr("l2_d", [128, 2, 6144], F32)

    dbg_outs = {}
    for k, shp in dbg.items():
        if k.startswith("_"):
            continue
        dbg_outs[k] = nc.dram_tensor("dbg_" + k, list(shp[0]), BF16 if shp[1] == "bf16" else F32,
                                     kind="ExternalOutput").ap()

    es = ExitStack()
    sb_bytes = [0]

    def sb(name, shape, dt=F32):
        n = 1
        for s_ in shape[1:]:
            n *= s_
        sb_bytes[0] += n * (4 if dt == F32 else 2)
        return es.enter_context(nc.sbuf_tensor(name, list(shape), dt))[:]

    def newsem(name):
        return es.enter_context(nc.semaphore(name))

    ident = sb("ident", [128, 128], BF16)
    mask4 = sb("mask4", [128, 512], BF16)
    ones3 = sb("ones3", [128, 4, 64], BF16)
    eps_t = sb("eps_t", [128, 1], F32)
    g1b = sb("g1b", [128, D], F32)
    g2b = sb("g2b", [128, D], F32)
    g3b = sb("g3b", [128, D], F32)
    bg = sb("bg_s", [128, 16], F32)
    caw = sb("caw_s", [128, 12], F32)
    cab = sb("cab_s", [128, 4], F32)
    fcw = sb("fcw_s", [128, 132], F32)
    fcb = sb("fcb_s", [128, 44], F32)
    vld0 = sb("vld0_s", [128, 40], F32)
    vld1 = sb("vld1_s", [128, 40], F32)
    vld2 = sb("vld2_s", [128, 64], F32)
    perm16 = sb("perm16_s", [128, 128], BF16)
    ssq = sb("ssq", [128, 12], F32)
    srt = sb("srt", [128, 12], F32)
    rstd = sb("rstd", [128, 12], F32)
    junk = sb("junk", [128, D], BF16)
    hb = [sb("hb%d" % i, [128, D], BF16) for i in range(2)]
    PT = [sb("PT%d" % i, [128, 512], BF16) for i in range(4)]
    uc = sb("uc", [128, 2, 44, 2], F32)
    corr = sb("corr", [128, 44, 2], F32)
    corr_t = sb("corr_t", [128, 44], F32)
    arena_t = sb("arena", [128, ARENA_BYTES // 2], BF16)
    A = Arena(arena_t)

    ps = [es.enter_context(nc.psum_tensor("ps%d" % i, [128, 512], F32))[:] for i in range(8)]
    tpv = [p.bitcast(BF16).rearrange("p (k c) -> p k c", k=8) for p in ps]
    ps_rr = [0]

    def next_ps():
        b = ps_rr[0] % 8
        ps_rr[0] += 1
        return b

    sem_const = newsem("s_const")

    const_ops = []

    def cdma(dst, src, key):
        const_ops.append(S.add("sp", lambda e: e.dma_start(out=dst, in_=src), writes=[key], dma_sem=sem_const))

    cdma(g1b, g1_d.partition_broadcast(128), "g1b")
    cdma(g2b, g2_d.partition_broadcast(128), "g2b")
    cdma(g3b, g3_d.partition_broadcast(128), "g3b")
    cdma(bg, bg_d[:, :], "bg")
    cdma(caw, caw_d[:, :], "caw")
    cdma(cab, cab_d[:, :], "cab")
    cdma(fcw, fcw_d[:, :], "fcw")
    cdma(fcb, fcb_d[:, :], "fcb")
    cdma(vld0, vld0_d[:, :], "vld0")
    cdma(vld1, vld1_d[:, :], "vld1")
    cdma(vld2, vld2_d[:, :], "vld2")
    for o_ in const_ops:
        o_.ev = const_ops[-1].ev

    identf = A.alloc([128, 128], F32)
    maskf = A.alloc([128, 256], F32)
    xs = [A.alloc([128, D], F32) for i in range(8)]
    hT2 = A.alloc([128, KC, 2048], BF16)
    W2 = A.alloc([128, KC, 768], BF16)
    K2T = [A.alloc([128, 2, 2048], BF16) for i in range(2)]
    Q2T = A.alloc([128, 2, 2048], BF16)
    V2A = [A.alloc([128, 16, 4, 128], BF16) for i in range(2)]
    U2s = A.alloc([128, 2, 2048], BF16)
    L2s = A.alloc([128, 2, 2048], F32)
    NCS = 3
    cst = [A.alloc([128, 1024], F32) for i in range(NCS)]
    csb = [A.alloc([128, 1024], BF16) for i in range(NCS)]
    sem_cst = [newsem("s_cst%d" % i) for i in range(NCS)]
    sem_cso = newsem("s_cso")
    print("phase-1 arena bytes:", A.off)
    sem_xs = [newsem("s_xs%d" % i) for i in range(8)]
    sem_w2 = newsem("s_w2")
    sem_ul = newsem("s_ul")

    S.add("pool", lambda e: e.memset(identf, 0.0), writes=["identf"])
    S.add("pool", lambda e: e.affine_select(out=identf, in_=identf, pattern=[[-1, 128]],
                                            compare_op=ALU.not_equal, fill=1.0, base=0,
                                            channel_multiplier=1), reads=["identf"], writes=["identf"])
    S.add("dve", lambda e: e.tensor_copy(out=ident, in_=identf), reads=["identf"], writes=["ident"])
    S.add("sp", lambda e: e.dma_start(out=identf, in_=perm_d[:, :]), reads=["ident"], writes=["identf"], dma_sem=sem_w2)
    S.add("dve", lambda e: e.tensor_copy(out=perm16, in_=identf), reads=["identf"], writes=["perm16"])
    S.add("pool", lambda e: e.memset(maskf, 1.0), writes=["maskf"])
    S.add("pool", lambda e: e.affine_select(out=maskf[:, 0:128], in_=maskf[:, 0:128], pattern=[[1, 128]],
                                            compare_op=ALU.is_ge, fill=0.0, base=0,
                                            channel_multiplier=-1), reads=["maskf"], writes=["maskf"])
    S.add("pool", lambda e: e.affine_select(out=maskf[:, 128:256], in_=maskf[:, 128:256], pattern=[[-1, 128]],
                                            compare_op=ALU.is_ge, fill=0.0, base=0,
                                            channel_multiplier=1), reads=["maskf"], writes=["maskf"])
    S.add("dve", lambda e: e.tensor_copy(out=mask4[:, 0:256], in_=maskf), reads=["maskf"], writes=["mask4"])
    S.add("dve", lambda e: e.tensor_copy(out=mask4[:, 256:512], in_=maskf), reads=["maskf"], writes=["mask4"])
    S.add("pool", lambda e: e.memset(ones3, 1.0), writes=["ones3"])
    S.add("pool", lambda e: e.memset(eps_t, EPS), writes=["eps"])
    S.add("pool", lambda e: e.memset(uc, 0.0), writes=[("uc", p_, ch_) for p_ in range(2) for ch_ in range(44)])

    def norm_stats(src_rows, nsub, sq_base, rkey):
        for s_ in range(nsub):
            col = sq_base + s_
            S.add("act", lambda e, s_=s_, col=col: e.activation(
                out=junk, in_=src_rows(s_), func=AF.Square, accum_out=ssq[:, col:col + 1]),
                reads=[rkey(s_)], writes=[("ssq", col)])
        cols = slice(sq_base, sq_base + nsub)
        S.add("act", lambda e: e.activation(out=srt[:, cols], in_=ssq[:, cols], func=AF.Sqrt,
                                            scale=1.0 / D, bias=eps_t),
              reads=[("ssq", c) for c in range(sq_base, sq_base + nsub)] + ["eps"],
              writes=[("srt", sq_base)])
        S.add("dve", lambda e: e.reciprocal(out=rstd[:, cols], in_=srt[:, cols]),
              reads=[("srt", sq_base)], writes=[("rstd", c) for c in range(sq_base, sq_base + nsub)])

    hb_rr = [0]

    def norm_transpose(src_rows, nsub, gb, gkey, sq_base, rkey, dst_fn, src_view, dst_keys):
        for s_ in range(nsub):
            i = hb_rr[0] % 2
            hb_rr[0] += 1
            col = sq_base + s_
            S.add("dve", lambda e, s_=s_, i=i, col=col: e.scalar_tensor_tensor(
                out=hb[i], in0=src_rows(s_), scalar=rstd[:, col:col + 1], in1=gb,
                op0=ALU.mult, op1=ALU.mult),
                reads=[rkey(s_), ("rstd", col), gkey], writes=[("hb", i)])
            t = next_ps()
            for kc in range(KC):
                S.add("pe", lambda e, i=i, t=t, kc=kc: e.transpose(
                    out=tpv[t][:, kc, :], in_=hb[i][:, kc * 128:(kc + 1) * 128], identity=ident),
                    reads=[("hb", i), "ident"], writes=[("ps", t)])
            S.add("act", lambda e, s_=s_, t=t: e.activation(out=dst_fn(s_), in_=src_view(tpv[t]), func=AF.Copy),
                  reads=[("ps", t)], writes=dst_keys(s_))

    def attention_run(calls, pt_rr):
        steps = [(ci, q2) for ci, c in enumerate(calls) for q2 in range(0, c["nblk"], 2)]
        st = {}

        def stage_a(k):
            ci, q2 = steps[k]
            c = calls[ci]
            bs = [next_ps(), next_ps()]
            nq = min(2, c["nblk"] - q2)
            for qq in range(nq):
                q = q2 + qq
                for kp, Kf in enumerate((c["Kcur"], c["Kprev"])):
                    for h in range(2):
                        S.add("pe", lambda e, b=bs[h], qq=qq, kp=kp, Kf=Kf, q=q, h=h, Qf=c["Qblk"]: e.matmul(
                            ps[b][:, qq * 256 + kp * 128: qq * 256 + kp * 128 + 128],
                            lhsT=Kf(q, h), rhs=Qf(q, h), start=True, stop=True, skip_group_check=True),
                            reads=c["kkeys"](q), writes=[("ps", bs[h])])
            pis = []
            for h in range(2):
                pi = pt_rr[0] % 4
                pt_rr[0] += 1
                pis.append(pi)
                w = nq * 256
                S.add("act", lambda e, b=bs[h], pi=pi, w=w: e.activation(
                    out=PT[pi][:, 0:w], in_=ps[b][:, 0:w], func=AF.Exp, scale=0.125),
                    reads=[("ps", bs[h])], writes=[("PT", pi)])
                S.add("dve", lambda e, pi=pi, w=w: e.tensor_tensor(
                    out=PT[pi][:, 0:w], in0=PT[pi][:, 0:w], in1=mask4[:, 0:w], op=ALU.mult),
                    reads=[("PT", pi), "mask4"], writes=[("PT", pi)])
            st[k] = (pis, nq)

        def stage_b(k):
            ci, q2 = steps[k]
            c = calls[ci]
            if "bo" not in c:
                c["bo"] = [next_ps(), next_ps()]
                c["first"] = [True, True]
            bo = c["bo"]
            pis, nq = st.pop(k)
            for qq in range(nq):
                q = q2 + qq
                for h in range(2):
                    for kp, Vf in enumerate((c["Vcur"], c["Vprev"])):
                        S.add("pe", lambda e, b=bo[h], q=q, qq=qq, h=h, kp=kp, Vf=Vf, pi=pis[h], st_=c["first"][h]: e.matmul(
                            ps[b][:, q * 128:(q + 1) * 128], lhsT=Vf(q, h),
                            rhs=PT[pi][:, qq * 256 + kp * 128: qq * 256 + kp * 128 + 128],
                            start=st_, stop=(kp == 1), skip_group_check=True),
                            reads=c["vkeys"](q) + [("PT", pis[h])], writes=[("ps", bo[h])])
                        c["first"][h] = False
            if q2 + 2 >= c["nblk"]:
                c["evac"](bo)

        if not steps:
            return
        stage_a(0)
        for k in range(len(steps)):
            if k + 1 < len(steps):
                stage_a(k + 1)
            stage_b(k)

    cs_rr = [0]
    cs_busy = [False] * NCS

    def cast_slot():
        i = cs_rr[0] % NCS
        cs_rr[0] += 1
        assert not cs_busy[i], "cast staging ring overflow"
        cs_busy[i] = True
        return i

    for kc in range(KC):
        i = cast_slot()
        ops_ = [S.add("sp", lambda e, i=i, kc=kc, a=a, c0=c0: e.dma_start(
            out=cst[i][:, a * 256:(a + 1) * 256], in_=w_in[kc * 128:(kc + 1) * 128, c0:c0 + 256]),
            reads=[("cst", i)], writes=[("cst", i, a)], dma_sem=sem_cst[i]) for a, c0 in enumerate((2048, 2816, 3584))]
        for o_ in ops_:
            o_.ev = ops_[-1].ev
        S.add("pool", lambda e, i=i, kc=kc: e.tensor_copy(out=W2[:, kc, :], in_=cst[i][:, 0:768]),
              reads=[("cst", i, 0), ("cst", i, 1), ("cst", i, 2)], writes=["W2", ("cst", i)])
        cs_busy[i] = False

    def cast_chunks():
        for (src, dst, K_, N_) in ((w_in, win_b, D, DIN), (w_pa, wpa_b, 512, D), (w_pb, wpb_b, 768, D),
                                   (w_out, wout_b, D, D), (w_up, wup_b, D, 2 * DFF), (w_dn, wdn_b, DFF, D)):
            for r0 in range(0, K_, 128):
                for c0 in range(0, N_, 1024):
                    n = min(1024, N_ - c0)
                    yield src[r0:r0 + 128, c0:c0 + n], dst[r0:r0 + 128, c0:c0 + n], n

    cast_iter = cast_chunks()
    cast_pending = []

    def cast_step(nload, keep=1):
        while len(cast_pending) > keep:
            i, dst, n = cast_pending.pop(0)
            S.add("sp", lambda e, i=i, dst=dst, n=n: e.dma_start(out=dst, in_=csb[i][:, 0:n]),
                  reads=[("csb", i)], dma_sem=sem_cso, barrier_dma=True)
            cs_busy[i] = False
        for _ in range(nload):
            nxt = next(cast_iter, None)
            if nxt is None:
                break
            src, dst, n = nxt
            i = cast_slot()
            S.add("sp", lambda e, i=i, src=src, n=n: e.dma_start(out=cst[i][:, 0:n], in_=src),
                  writes=[("cst", i)], dma_sem=sem_cst[i])
            S.add("pool", lambda e, i=i, n=n: e.tensor_copy(out=csb[i][:, 0:n], in_=cst[i][:, 0:n]),
                  reads=[("cst", i)], writes=[("csb", i)])
            cast_pending.append((i, dst, n))

    for _ in range(200):
        cast_step(2)
    cast_step(0, keep=0)
    assert next(cast_iter, None) is None and not cast_pending
    xs_rr = [0]
    pt_rr = [0]
    pre_q = []
    nsb = dbg.get("_nsb", 4)
    for s in range(nsb):
        par = s % 2
        def p1_load(qt, sb_=None):
            sb_ = s if sb_ is None else sb_
            slots = []
            for sub in range(4):
                i = xs_rr[0] % 8
                xs_rr[0] += 1
                slots.append(i)
                tok0 = sb_ * 2048 + qt * 512 + sub * 128
                S.add("aq", lambda e, i=i, tok0=tok0: e.dma_start(out=xs[i], in_=x_loc[tok0:tok0 + 128, :]),
                      writes=[("xs", i)], dma_sem=sem_xs[i])
            return slots

        def p1_stats(qt, slots):
            norm_stats(lambda s_, slots=slots: xs[slots[s_]], 4, (qt % 2) * 4, lambda s_, slots=slots: ("xs", slots[s_]))

        def p1_nt(qt, slots, nsub=4):
            sqb = (qt % 2) * 4
            for s_ in range(nsub):
                i = hb_rr[0] % 2
                hb_rr[0] += 1
                col = sqb + s_
                sl = slots[s_]
                S.add("dve", lambda e, i=i, col=col, sl=sl: e.scalar_tensor_tensor(
                    out=hb[i], in0=xs[sl], scalar=rstd[:, col:col + 1], in1=g1b, op0=ALU.mult, op1=ALU.mult),
                    reads=[("xs", sl), ("rstd", col), "g1b"], writes=[("hb", i)])
                j0 = qt * 32 + s_ * 8
                for half in range(2):
                    t = next_ps()
                    for k4 in range(4):
                        kc = half * 4 + k4
                        S.add("pe", lambda e, i=i, t=t, kc=kc, k4=k4: e.matmul(
                            ps[t][:, k4 * 128:(k4 + 1) * 128], lhsT=hb[i][:, kc * 128:(kc + 1) * 128], rhs=perm16,
                            start=True, stop=True, skip_group_check=True),
                            reads=[("hb", i), "perm16"], writes=[("ps", t)])
                    dst = hT2[:, half * 4:half * 4 + 4, :].rearrange("p k (r j) -> p k r j", r=16)[:, :, :, j0:j0 + 8]
                    src = ps[t].rearrange("p (k r j) -> p k r j", k=4, r=16)
                    if half == 0:
                        S.add("act", lambda e, dst=dst, src=src: e.activation(out=dst, in_=src, func=AF.Copy),
                              reads=[("ps", t)], writes=[("hT2", qt, 0)])
                    else:
                        S.add("dve", lambda e, dst=dst, src=src: e.tensor_copy(out=dst, in_=src),
                              reads=[("ps", t)], writes=[("hT2", qt, 1)])

        if s == 3:
            i = xs_rr[0] % 8
            xs_rr[0] += 1
            S.add("aq", lambda e, i=i: e.dma_start(out=xs[i], in_=x_loc[6144:6272, :]),
                  writes=[("xs", i)], dma_sem=sem_xs[i])
            norm_stats(lambda s_, i=i: xs[i], 1, 0, lambda s_, i=i: ("xs", i))
            p1_nt(0, [i], nsub=1)
            hT2_keys = [("hT2", 0, 0), ("hT2", 0, 1)]
            hv = lambda t: t.rearrange("p (r j) -> p r j", r=16)[:, :, 0:8]
            for pair in range(2):
                for (dst, dkey, wc0) in ((K2T[par], ("K2T", par), 256), (Q2T, ("Q2T",), 0)):
                    b = next_ps()
                    for kc in range(KC):
                        S.add("pe", lambda e, b=b, kc=kc, wc0=wc0, pair=pair: e.matmul(
                            ps[b][:, 0:128], lhsT=W2[:, kc, wc0 + pair * 128: wc0 + pair * 128 + 128],
                            rhs=hv(hT2[:, kc, :]), start=(kc == 0), stop=(kc == KC - 1)),
                            reads=["W2"] + hT2_keys, writes=[("ps", b)])
                    S.add("act", lambda e, b=b, dst=dst, pair=pair: e.activation(
                        out=hv(dst[:, pair, :]), in_=ps[b][:, 0:128].rearrange("p (r j) -> p r j", r=16), func=AF.Copy),
                        reads=[("ps", b)], writes=[dkey + (pair, cq) for cq in range(4)])
            for r2 in range(8):
                b = next_ps()
                for rr in range(2):
                    r = r2 * 2 + rr
                    for kc in range(KC):
                        S.add("pe", lambda e, b=b, kc=kc, r=r, rr=rr: e.matmul(
                            ps[b][0:8, rr * 256:(rr + 1) * 256], lhsT=hT2[:, kc, r * 128:r * 128 + 8],
                            rhs=W2[:, kc, 512:768], start=(kc == 0), stop=(kc == KC - 1), skip_group_check=True),
                            reads=["W2"] + hT2_keys, writes=[("ps", b)])
                S.add("act", lambda e, b=b, r2=r2, par=par: e.activation(
                    out=V2A[par][0:8, r2 * 2:r2 * 2 + 2, :, 0:64],
                    in_=ps[b][0:8, :].rearrange("p (r h e) -> p r h e", r=2, h=4), func=AF.Copy),
                    reads=[("ps", b)], writes=[("V2A", par, r2)])
                for rr in range(2):
                    r = r2 * 2 + rr
                    S.add("dve", lambda e, r=r, par=par, s=s: e.tensor_scalar(
                        out=V2A[par][:, r, :, 64:128], in0=ones3, scalar1=vld2[:, s * 16 + r:s * 16 + r + 1],
                        scalar2=None, op0=ALU.mult),
                        reads=["ones3", "vld2"], writes=[("V2A1", par, r)])
        else:
            if not pre_q:
                pre_q.extend([p1_load(0), p1_load(1)])
            sl_q = list(pre_q)
            del pre_q[:]
            p1_stats(0, sl_q[0])
            for qt in range(4):
                if qt + 1 < 4:
                    p1_stats(qt + 1, sl_q[qt + 1])
                p1_nt(qt, sl_q[qt])
                if qt + 2 < 4:
                    sl_q.append(p1_load(qt + 2))
                elif s + 1 < min(nsb, 3):
                    pre_q.append(p1_load(qt + 2 - 4, s + 1))
        hT2_keys = [("hT2", q, hh_) for q in range(4) for hh_ in range(2)]
        for pair in (range(2) if s < 3 else ()):
            for (dst, dkey, wc0, need) in ((K2T[par], ("K2T", par), 256, True), (Q2T, ("Q2T",), 0, s >= 1)):
                if not need:
                    continue
                for cq in range(4):
                    b = next_ps()
                    for kc in range(KC):
                        S.add("pe", lambda e, b=b, kc=kc, wc0=wc0, pair=pair, cq=cq: e.matmul(
                            ps[b], lhsT=W2[:, kc, wc0 + pair * 128: wc0 + pair * 128 + 128],
                            rhs=hT2[:, kc, cq * 512:(cq + 1) * 512], start=(kc == 0), stop=(kc == KC - 1)),
                            reads=["W2"] + hT2_keys, writes=[("ps", b)])
                    if cq % 2 == 0:
                        S.add("act", lambda e, b=b, dst=dst, pair=pair, cq=cq: e.activation(
                            out=dst[:, pair, cq * 512:(cq + 1) * 512], in_=ps[b], func=AF.Copy),
                            reads=[("ps", b)], writes=[dkey + (pair, cq)])
                    else:
                        S.add("dve", lambda e, b=b, dst=dst, pair=pair, cq=cq: e.tensor_copy(
                            out=dst[:, pair, cq * 512:(cq + 1) * 512], in_=ps[b]),
                            reads=[("ps", b)], writes=[dkey + (pair, cq)])
        for r2 in (range(8) if s < 3 else ()):
            b = next_ps()
            for rr in range(2):
                r = r2 * 2 + rr
                for kc in range(KC):
                    S.add("pe", lambda e, b=b, kc=kc, r=r, rr=rr: e.matmul(
                        ps[b][:, rr * 256:(rr + 1) * 256], lhsT=hT2[:, kc, r * 128:(r + 1) * 128],
                        rhs=W2[:, kc, 512:768], start=(kc == 0), stop=(kc == KC - 1), skip_group_check=True),
                        reads=["W2"] + hT2_keys, writes=[("ps", b)])
            S.add("act", lambda e, b=b, r2=r2, par=par: e.activation(
                out=V2A[par][:, r2 * 2:r2 * 2 + 2, :, 0:64],
                in_=ps[b].rearrange("p (r h e) -> p r h e", r=2, h=4), func=AF.Copy),
                reads=[("ps", b)], writes=[("V2A", par, r2)])
            for rr in range(2):
                r = r2 * 2 + rr
                S.add("dve", lambda e, r=r, par=par, s=s: e.tensor_scalar(
                    out=V2A[par][:, r, :, 64:128], in0=ones3, scalar1=vld2[:, s * 16 + r:s * 16 + r + 1],
                    scalar2=None, op0=ALU.mult),
                    reads=["ones3", "vld2"], writes=[("V2A1", par, r)])
        if s == 0:
            continue
        calls = []
        hsl = lambda h: slice(h * 64, (h + 1) * 64)
        for pair in range(2):
            for r4 in range(4):
                blk = lambda q, r4=r4: slice((r4 * 4 + q) * 128, (r4 * 4 + q + 1) * 128)

                def evac(bo, pair=pair, r4=r4):
                    for h in range(2):
                        hs = slice(h * 64, (h + 1) * 64)
                        src_u = ps[bo[h]][0:64, :].rearrange("p (r j) -> p j r", r=4)
                        src_l = ps[bo[h]][64:128, :].rearrange("p (r j) -> p j r", r=4)
                        du = U2s[hs, pair, :].rearrange("p (j r) -> p j r", r=16)[:, :, r4 * 4:r4 * 4 + 4]
                        dl = L2s[hs, pair, :].rearrange("p (j r) -> p j r", r=16)[:, :, r4 * 4:r4 * 4 + 4]
                        S.add("act", lambda e, src_u=src_u, du=du: e.activation(out=du, in_=src_u, func=AF.Copy),
                              reads=[("ps", bo[h])], writes=[("U2s", pair, r4, h)])
                        S.add("dve", lambda e, src_l=src_l, dl=dl: e.tensor_copy(out=dl, in_=src_l),
                              reads=[("ps", bo[h]), ("U2s", pair, r4, h)], writes=[("L2s", pair, r4, h)])
                calls.append(dict(
                    nblk=4,
                    Kcur=lambda q, h, pair=pair, blk=blk, par=par: K2T[par][hsl(h), pair, blk(q)],
                    Kprev=lambda q, h, pair=pair, blk=blk, par=par: K2T[1 - par][hsl(h), pair, blk(q)],
                    Qblk=lambda q, h, pair=pair, blk=blk: Q2T[hsl(h), pair, blk(q)],
                    Vcur=lambda q, h, pair=pair, r4=r4, par=par: V2A[par][:, r4 * 4 + q, pair * 2 + h, :],
                    Vprev=lambda q, h, pair=pair, r4=r4, par=par: V2A[1 - par][:, r4 * 4 + q, pair * 2 + h, :],
                    kkeys=lambda q, pair=pair, r4=r4, par=par: [("K2T", par, pair, r4), ("K2T", 1 - par, pair, r4),
                                                              ("Q2T", pair, r4)],
                    vkeys=lambda q, r4=r4, par=par: [("V2A", par, (r4 * 4 + q) // 2), ("V2A1", par, r4 * 4 + q),
                                                    ("V2A", 1 - par, (r4 * 4 + q) // 2), ("V2A1", 1 - par, r4 * 4 + q)],
                    evac=evac))
        attention_run(calls, pt_rr)
        base = (s - 1) * 2048
        S.add("aq", lambda e, base=base: e.dma_start(out=u2_d[:, :, base:base + 2048], in_=U2s),
              reads=[("U2s", p_, r_, h_) for p_ in range(2) for r_ in range(4) for h_ in range(2)],
              writes=[("u2_d", s)], dma_sem=sem_ul, barrier_dma=True)
        S.add("aq", lambda e, base=base: e.dma_start(out=l2_d[:, :, base:base + 2048], in_=L2s),
              reads=[("L2s", p_, r_, h_) for p_ in range(2) for r_ in range(4) for h_ in range(2)],
              writes=[("l2_d", s)], dma_sem=sem_ul, barrier_dma=True)

    S.barrier()
    A.reset()
    xt = [A.alloc([128, 4, D], F32) for i in range(2)]
    hT = A.alloc([128, KC, 512], BF16)
    cv = A.alloc([128, 4, 514], F32)
    tmp = [A.alloc([128, 512], F32) for i in range(6)]
    yaT = A.alloc([128, 4, 512], BF16)
    QT01 = A.alloc([128, 4, 512], BF16)
    K01T = [A.alloc([128, 4, 512], BF16) for i in range(2)]
    V01A = [A.alloc([128, 8, 4, 128], BF16) for i in range(2)]
    U01 = A.alloc([128, 4, 512], BF16)
    U2t = [A.alloc([128, 2, 512], BF16) for i in range(2)]
    L2t = [A.alloc([128, 2, 512], F32) for i in range(2)]
    Zacc = A.alloc([128, 2, 512], F32)
    mT = A.alloc([128, 8, 512], BF16)
    actT = A.alloc([128, 22, 512], BF16)
    WS = [A.alloc([128, KC * 512], BF16) for i in range(NWS)]
    print("phase-2 arena bytes:", A.off, " persistent bytes:", sb_bytes[0] - ARENA_BYTES)
    assert A.hi <= ARENA_BYTES, (A.hi, ARENA_BYTES)
    sem_ws = [newsem("s_ws%d" % i) for i in range(NWS)]
    sem_xt = [newsem("s_xt%d" % i) for i in range(2)]
    sem_ult = [newsem("s_ult%d" % i) for i in range(2)]
    sem_out = newsem("s_out")
    sem_dbg = newsem("s_dbg")

    S.add("pool", lambda e: e.memset(cv, 0.0), writes=[("cv", c) for c in range(4)])

    ws_rr = [0]
    ws_free = [True] * NWS

    def load_slab(src_ap, kc_n, width, ckey):
        for t_ in range(NWS):
            i = (ws_rr[0] + t_) % NWS
            if ws_free[i]:
                break
        else:
            raise AssertionError("weight slab ring overflow")
        ws_rr[0] = i + 1
        ws_free[i] = False
        dst = WS[i][:, 0:kc_n * width].rearrange("p (k c) -> p k c", k=kc_n)
        S.add("sp", lambda e: e.dma_start(out=dst, in_=src_ap.rearrange("(k p) c -> p k c", p=128)),
              writes=[("ws", i)], dma_sem=sem_ws[i])
        return (dst, i)

    def release(slab):
        ws_free[slab[1]] = True

    def xt_load(j):
        par = j % 2
        tok0 = 2048 + 512 * j
        S.add("sp", lambda e: e.dma_start(out=xt[par], in_=x_loc[tok0:tok0 + 512, :].rearrange(
            "(s p) d -> p s d", p=128)), writes=[("xt", par, s_) for s_ in range(4)], dma_sem=sem_xt[par])

    def ul_load(j):
        par = j % 2
        n = 512 if j < 8 else 128
        o1 = S.add("sp", lambda e: e.dma_start(out=U2t[par][:, :, 0:n], in_=u2_d[:, :, 512 * j:512 * j + n]),
                   reads=[("u2_d", s_) for s_ in range(1, nsb)], writes=[("U2t", par)], dma_sem=sem_ult[par])
        o2 = S.add("sp", lambda e: e.dma_start(out=L2t[par][:, :, 0:n], in_=l2_d[:, :, 512 * j:512 * j + n]),
                   reads=[("l2_d", s_) for s_ in range(1, nsb)], writes=[("L2t", par)], dma_sem=sem_ult[par])
        o1.ev = o2.ev

    def fm_group(slab, kc_n, col0, rhs_fn, rkeys, N=512):
        b = next_ps()
        for kc in range(kc_n):
            S.add("pe", lambda e, b=b, kc=kc: e.matmul(
                ps[b][:, 0:N], lhsT=slab[0][:, kc, col0:col0 + 128], rhs=rhs_fn(kc),
                start=(kc == 0), stop=(kc == kc_n - 1)),
                reads=[("ws", slab[1])] + rkeys, writes=[("ps", b)])
        return b

    hT_keys = [("hT", s_) for s_ in range(4)]
    tmp_rr = [0]

    def next_tmp():
        i = tmp_rr[0] % 6
        tmp_rr[0] += 1
        return i

    pending_stores = []

    def flush_stores():
        for f in pending_stores:
            f()
        del pending_stores[:]

    ntile = dbg.get("_ntiles", NT2)

    def n1(j):
        par = j % 2
        xsrc = xt[par]
        xk = lambda s_: ("xt", par, s_)
        TW = 512 if j < 8 else 128
        NS = TW // 128
        NS = 4 if j < 8 else 1
        norm_stats(lambda s_: xsrc[:, s_, :], NS, 0, xk)
        norm_transpose(lambda s_: xsrc[:, s_, :], NS, g1b, "g1b", 0, xk,
                       lambda s_: hT[:, :, s_ * 128:(s_ + 1) * 128], lambda v: v, lambda s_: [("hT", s_)])

    def tile(j, mid_hook=None):
        par = j % 2
        pre = (j < 0)
        xsrc = xt[par]
        xk = lambda s_: ("xt", par, s_)
        TW = 512 if j < 8 else 128
        NS = TW // 128
        for (c0, dst, dk, need) in ((2304, K01T[par], ("K01T", par), True), (1536, QT01, ("QT01",), not pre)):
            if not need:
                continue
            sl = load_slab(win_b[:, c0:c0 + 512], KC, 512, "win")
            for c in range(4):
                b = fm_group(sl, KC, c * 128, lambda kc: hT[:, kc, 0:TW], hT_keys, TW)
                if c < 2:
                    S.add("act", lambda e, b=b, dst=dst, c=c: e.activation(out=dst[:, c, 0:TW], in_=ps[b][:, 0:TW], func=AF.Copy),
                          reads=[("ps", b)], writes=[dk + (c,)])
                else:
                    S.add("act", lambda e, b=b, dst=dst, c=c: e.activation(
                        out=dst[:, c, :].rearrange("p (r i) -> p r i", r=4)[:, :, 0:TW // 4],
                        in_=ps[b][:, 0:TW].rearrange("p (i r) -> p r i", r=4), func=AF.Copy),
                        reads=[("ps", b)], writes=[dk + (c,)])
            release(sl)
        slV = load_slab(win_b[:, 3072:3584], KC, 512, "win")
        for g in range(2):
            for k2 in range(2):
                if TW < 512 and g == 0 and k2 == 1:
                    continue
                b = next_ps()
                nk = 1 if (TW < 512 and g == 0) else 2
                MV = 128 if (TW == 512 or g == 0) else TW // 4
                for kk in range(nk):
                    kb = k2 * 2 + kk
                    for kc in range(KC):
                        if g == 0:
                            lhs = hT[:, kc, kb * 128:(kb + 1) * 128]
                        else:
                            lhs = hT[:, kc, 0:TW].rearrange("p (i r) -> p r i", r=4)[:, kb, :]
                        S.add("pe", lambda e, b=b, kk=kk, kc=kc, lhs=lhs, g=g, MV=MV: e.matmul(
                            ps[b][0:MV, kk * 256:(kk + 1) * 256], lhsT=lhs, rhs=slV[0][:, kc, g * 256:(g + 1) * 256],
                            start=(kc == 0), stop=(kc == KC - 1), skip_group_check=True),
                            reads=[("ws", slV[1])] + hT_keys, writes=[("ps", b)])
                S.add("act", lambda e, b=b, g=g, k2=k2, nk=nk, MV=MV: e.activation(
                    out=V01A[par][0:MV, g * 4 + k2 * 2: g * 4 + k2 * 2 + nk, :, 0:64],
                    in_=ps[b][0:MV, 0:nk * 256].rearrange("p (r h e) -> p r h e", r=nk, h=4), func=AF.Copy),
                    reads=[("ps", b)], writes=[("V01A", par, g, k2)])
                for kk in range(2):
                    kb = k2 * 2 + kk
                    vt = vld0 if g == 0 else vld1
                    col = (j + 1) * 4 + kb
                    S.add("dve", lambda e, g=g, kb=kb, vt=vt, col=col: e.tensor_scalar(
                        out=V01A[par][:, g * 4 + kb, :, 64:128], in0=ones3, scalar1=vt[:, col:col + 1],
                        scalar2=None, op0=ALU.mult),
                        reads=["ones3", "vld0", "vld1"], writes=[("V01A1", par, g, kb)])
        release(slV)
        flush_stores()
        if pre:
            return
        slC = load_slab(win_b[:, 512:1024], KC, 512, "win")
        slVv = load_slab(win_b[:, 1024:1536], KC, 512, "win")
        slB = load_slab(win_b[:, 0:512], KC, 512, "win")
        for c in range(4):
            b1 = fm_group(slC, KC, c * 128, lambda kc: hT[:, kc, 0:TW], hT_keys, TW)
            t1 = next_tmp()
            S.add("act", lambda e, b1=b1, t1=t1: e.activation(out=tmp[t1][:, 0:TW], in_=ps[b1][:, 0:TW], func=AF.Copy),
                  reads=[("ps", b1)], writes=[("tmp", t1)])
            b2 = fm_group(slVv, KC, c * 128, lambda kc: hT[:, kc, 0:TW], hT_keys, TW)
            S.add("dve", lambda e, b2=b2, t1=t1, c=c: e.tensor_tensor(out=cv[:, c, 2:2 + TW], in0=ps[b2][:, 0:TW],
                                                                     in1=tmp[t1][:, 0:TW], op=ALU.mult),
                  reads=[("ps", b2), ("tmp", t1)], writes=[("cv", c)])
            t2 = next_tmp()
            S.add("dve", lambda e, t2=t2, c=c: e.tensor_scalar(
                out=tmp[t2][:, 0:TW], in0=cv[:, c, 2:2 + TW], scalar1=caw[:, c * 3 + 2:c * 3 + 3], scalar2=cab[:, c:c + 1],
                op0=ALU.mult, op1=ALU.add), reads=[("cv", c), "caw", "cab"], writes=[("tmp", t2)])
            S.add("dve", lambda e, t2=t2, c=c: e.scalar_tensor_tensor(
                out=tmp[t2][:, 0:TW], in0=cv[:, c, 1:1 + TW], scalar=caw[:, c * 3 + 1:c * 3 + 2], in1=tmp[t2][:, 0:TW],
                op0=ALU.mult, op1=ALU.add), reads=[("cv", c), "caw", ("tmp", t2)], writes=[("tmp", t2)])
            S.add("dve", lambda e, t2=t2, c=c: e.scalar_tensor_tensor(
                out=tmp[t2][:, 0:TW], in0=cv[:, c, 0:TW], scalar=caw[:, c * 3:c * 3 + 1], in1=tmp[t2][:, 0:TW],
                op0=ALU.mult, op1=ALU.add), reads=[("cv", c), "caw", ("tmp", t2)], writes=[("tmp", t2)])
            S.add("pool", lambda e, c=c: e.tensor_copy(out=cv[:, c, 0:2], in_=cv[:, c, TW:TW + 2]),
                  reads=[("cv", c)], writes=[("cv", c)])
            b3 = fm_group(slB, KC, c * 128, lambda kc: hT[:, kc, 0:TW], hT_keys, TW)
            S.add("dve", lambda e, b3=b3, t2=t2, c=c: e.tensor_tensor(out=yaT[:, c, 0:TW], in0=ps[b3][:, 0:TW],
                                                                     in1=tmp[t2][:, 0:TW], op=ALU.mult),
                  reads=[("ps", b3), ("tmp", t2)], writes=[("yaT", c)])
        release(slC)
        release(slVv)
        release(slB)
        hsl = lambda h: slice(h * 64, (h + 1) * 64)
        blk = lambda q: slice(q * 128, (q + 1) * 128)
        calls = []
        for g in range(2):
            for pair in range(2):
                c = g * 2 + pair
                if g == 0:
                    Kprev = lambda q, h, c=c: (K01T[par][hsl(h), c, blk(q - 1)] if q > 0
                                               else K01T[1 - par][hsl(h), c, blk(3)])
                    Vprev = lambda q, h, pair=pair: (V01A[par][:, q - 1, pair * 2 + h, :] if q > 0
                                                     else V01A[1 - par][:, 3, pair * 2 + h, :])
                    vkeys = lambda q: ([("V01A", par, 0, q // 2), ("V01A1", par, 0, q)] +
                                       ([("V01A", par, 0, (q - 1) // 2), ("V01A1", par, 0, q - 1)] if q > 0 else
                                        [("V01A", 1 - par, 0, 1), ("V01A1", 1 - par, 0, 3)]))
                else:
                    Kprev = lambda q, h, c=c: K01T[1 - par][hsl(h), c, blk(q)]
                    Vprev = lambda q, h, pair=pair: V01A[1 - par][:, 4 + q, pair * 2 + h, :]
                    vkeys = lambda q: [("V01A", par, 1, q // 2), ("V01A1", par, 1, q),
                                       ("V01A", 1 - par, 1, q // 2), ("V01A1", 1 - par, 1, q)]

                def evac(bo, g=g, pair=pair, c=c):
                    for h in range(2):
                        hs = hsl(h)
                        if g == 0:
                            su, sl_ = ps[bo[h]][0:64, 0:TW], ps[bo[h]][64:128, 0:TW]
                            du = U01[hs, c, 0:TW]
                            dz = Zacc[hs, pair, 0:TW]
                        else:
                            su = ps[bo[h]][0:64, :].rearrange("p (r i) -> p i r", r=4)[:, 0:TW // 4, :]
                            sl_ = ps[bo[h]][64:128, :].rearrange("p (r i) -> p i r", r=4)[:, 0:TW // 4, :]
                            du = U01[hs, c, 0:TW].rearrange("p (i r) -> p i r", r=4)
                            dz = Zacc[hs, pair, 0:TW].rearrange("p (i r) -> p i r", r=4)
                        S.add("act", lambda e, su=su, du=du: e.activation(out=du, in_=su, func=AF.Copy),
                              reads=[("ps", bo[h])], writes=[("U01", c, h)])
                        if g == 0:
                            n = 512 if j < 8 else 128
                            S.add("dve", lambda e, sl_=sl_, dz=dz: e.tensor_copy(out=dz, in_=sl_),
                                  reads=[("ps", bo[h]), ("U01", c, h)], writes=[("Zacc", pair, h)])
                            S.add("dve", lambda e, hs=hs, pair=pair, n=n: e.tensor_tensor(
                                out=Zacc[hs, pair, 0:n], in0=Zacc[hs, pair, 0:n], in1=L2t[par][hs, pair, 0:n],
                                op=ALU.add), reads=[("Zacc", pair, h), ("L2t", par)], writes=[("Zacc", pair, h)])
                        else:
                            S.add("dve", lambda e, sl_=sl_, dz=dz: e.tensor_tensor(out=dz, in0=sl_, in1=dz, op=ALU.add),
                                  reads=[("ps", bo[h]), ("Zacc", pair, h), ("U01", c, h)], writes=[("Zacc", pair, h)])
                calls.append(dict(
                    nblk=(4 if (g == 1 or TW == 512) else 1),
                    Kcur=lambda q, h, c=c: K01T[par][hsl(h), c, blk(q)],
                    Kprev=Kprev,
                    Qblk=lambda q, h, c=c: QT01[hsl(h), c, blk(q)],
                    Vcur=lambda q, h, pair=pair, g=g: V01A[par][:, g * 4 + q, pair * 2 + h, :],
                    Vprev=Vprev,
                    kkeys=lambda q, c=c: [("K01T", par, c), ("K01T", 1 - par, c), ("QT01", c)],
                    vkeys=vkeys, evac=evac))
        attention_run(calls, pt_rr)
        zk = [("Zacc", p_, h) for p_ in range(2) for h in range(2)]
        S.add("dve", lambda e: e.tensor_scalar(out=Zacc[:, :, 0:TW], in0=Zacc[:, :, 0:TW], scalar1=1e-30, scalar2=None,
                                               op0=ALU.add), reads=zk, writes=zk)
        S.add("dve", lambda e: e.reciprocal(out=Zacc[:, :, 0:TW], in_=Zacc[:, :, 0:TW]), reads=zk, writes=zk)
        n2 = 512 if j < 8 else 128
        for c in range(6):
            if c < 4:
                src, rk, n = U01[:, c, 0:TW], [("U01", c, 0), ("U01", c, 1)], TW
            else:
                src, rk, n = U2t[par][:, c - 4, 0:n2], [("U2t", par)], n2
            eng = "pool" if c % 2 == 0 else "dve"
            S.add(eng, lambda e, c=c, src=src, n=n: e.tensor_tensor(out=src, in0=src, in1=Zacc[:, c % 2, 0:n],
                                                                  op=ALU.mult),
                  reads=rk + zk, writes=[("yb", c)] + rk)
        yb = lambda kc: (U01[:, kc, 0:TW] if kc < 4 else U2t[par][:, kc - 4, 0:TW])
        ybk = [("yb", c) for c in range(6)] + [("U01", c, h) for c in range(4) for h in range(2)] + [("U2t", par)]
        if j + 1 < ntile:
            xt_load(j + 1)
            ul_load(j + 1)
        slPA = load_slab(wpa_b[:, :], 4, 1024, "wpa")
        yak = [("yaT", c) for c in range(4)]
        for hf in range(2):
            slGA = load_slab(win_b[:, 3840 + hf * 512: 3840 + (hf + 1) * 512], KC, 512, "win")
            for o4 in range(4):
                oc = hf * 4 + o4
                bga = fm_group(slGA, KC, o4 * 128, lambda kc: hT[:, kc, 0:TW], hT_keys, TW)
                ta = next_tmp()
                S.add("act", lambda e, bga=bga, ta=ta, oc=oc: e.activation(out=tmp[ta][:, 0:TW], in_=ps[bga][:, 0:TW],
                                                                          func=AF.Sigmoid, bias=bg[:, oc:oc + 1]),
                      reads=[("ps", bga), "bg"], writes=[("tmp", ta)])
                bpa = fm_group(slPA, 4, oc * 128, lambda kc: yaT[:, kc, 0:TW], yak, TW)
                S.add("dve", lambda e, bpa=bpa, ta=ta: e.tensor_tensor(out=tmp[ta][:, 0:TW], in0=ps[bpa][:, 0:TW],
                                                                      in1=tmp[ta][:, 0:TW], op=ALU.mult),
                      reads=[("ps", bpa), ("tmp", ta)], writes=[("tmp", ta)])
                S.add("pool", lambda e, ta=ta, oc=oc: e.tensor_copy(out=mT[:, oc, 0:TW], in_=tmp[ta][:, 0:TW]),
                      reads=[("tmp", ta)], writes=[("mT", oc)])
            release(slGA)
            slGB = load_slab(win_b[:, 4864 + hf * 512: 4864 + (hf + 1) * 512], KC, 512, "win")
            slPB = load_slab(wpb_b[:, hf * 512:(hf + 1) * 512], 6, 512, "wpb")
            for o4 in range(4):
                oc = hf * 4 + o4
                bgb = fm_group(slGB, KC, o4 * 128, lambda kc: hT[:, kc, 0:TW], hT_keys, TW)
                tb = next_tmp()
                S.add("act", lambda e, bgb=bgb, tb=tb, oc=oc: e.activation(out=tmp[tb][:, 0:TW], in_=ps[bgb][:, 0:TW],
                                                                          func=AF.Sigmoid, bias=bg[:, 8 + oc:8 + oc + 1]),
                      reads=[("ps", bgb), "bg"], writes=[("tmp", tb)])
                bpb = fm_group(slPB, 6, o4 * 128, yb, ybk, TW)
                S.add("dve", lambda e, bpb=bpb, tb=tb: e.tensor_tensor(out=tmp[tb][:, 0:TW], in0=ps[bpb][:, 0:TW],
                                                                      in1=tmp[tb][:, 0:TW], op=ALU.mult),
                      reads=[("ps", bpb), ("tmp", tb)], writes=[("tmp", tb)])
                S.add("pool", lambda e, tb=tb, oc=oc: e.tensor_tensor(out=mT[:, oc, 0:TW], in0=mT[:, oc, 0:TW],
                                                                     in1=tmp[tb][:, 0:TW], op=ALU.add),
                      reads=[("tmp", tb), ("mT", oc)], writes=[("mT", oc)])
            release(slGB)
            release(slPB)
        release(slPA)
        mk = [("mT", oc) for oc in range(8)]
        slo = [load_slab(wout_b[:, hf * 512:(hf + 1) * 512], KC, 512, "wout") for hf in range(2)]
        for s_ in range(NS):
            for hf in range(2):
                sl = slo[hf]
                b = next_ps()
                for kc in range(KC):
                    S.add("pe", lambda e, b=b, kc=kc, s_=s_, sl=sl: e.matmul(
                        ps[b], lhsT=mT[:, kc, s_ * 128:(s_ + 1) * 128], rhs=sl[0][:, kc, :],
                        start=(kc == 0), stop=(kc == KC - 1)),
                        reads=[("ws", sl[1])] + mk, writes=[("ps", b)])
                S.add("dve", lambda e, b=b, s_=s_, hf=hf: e.tensor_tensor(
                    out=xsrc[:, s_, hf * 512:(hf + 1) * 512], in0=ps[b], in1=xsrc[:, s_, hf * 512:(hf + 1) * 512],
                    op=ALU.add), reads=[("ps", b), xk(s_)], writes=[xk(s_)])
        release(slo[0])
        release(slo[1])
        if "x1" in dbg_outs and j == dbg.get("_tile", 0):
            S.add("sp", lambda e: e.dma_start(out=dbg_outs["x1"].rearrange("(s p) d -> p s d", p=128), in_=xsrc),
                  reads=[xk(s_) for s_ in range(4)], dma_sem=sem_dbg)
            for nm, ap_, rk_ in (("ya", yaT, yak), ("yb", U01, ybk), ("y2", U2t[par], ybk), ("mT", mT, mk),
                                 ("z", Zacc, zk)):
                if nm in dbg_outs:
                    S.add("sp", lambda e, nm=nm, ap_=ap_: e.dma_start(out=dbg_outs[nm], in_=ap_),
                          reads=rk_, dma_sem=sem_dbg)
        for s1 in range(NS):
            norm_stats(lambda s_, s1=s1: xsrc[:, s1, :], 1, 4 + s1, lambda s_, s1=s1: xk(s1))
        norm_transpose(lambda s_: xsrc[:, s_, :], NS, g2b, "g2b", 4, xk,
                       lambda s_: hT[:, :, s_ * 128:(s_ + 1) * 128], lambda v: v, lambda s_: [("hT", s_)])
        fw = fcw.rearrange("p (c k) -> p c k", k=3)
        ucp = uc[:, 1 - par, :, :]
        uck = [("uc", 1 - par, ch_) for ch_ in range(44)]
        S.add("pool", lambda e: e.tensor_tensor(out=corr[:, :, 0], in0=ucp[:, :, 1], in1=fw[:, :, 1], op=ALU.mult),
              reads=uck + ["fcw"], writes=["corr"])
        S.add("pool", lambda e: e.tensor_tensor(out=corr_t, in0=ucp[:, :, 0], in1=fw[:, :, 0], op=ALU.mult),
              reads=uck + ["fcw"], writes=["corr_t"])
        S.add("pool", lambda e: e.tensor_tensor(out=corr[:, :, 0], in0=corr[:, :, 0], in1=corr_t, op=ALU.add),
              reads=["corr", "corr_t"], writes=["corr"])
        S.add("pool", lambda e: e.tensor_tensor(out=corr[:, :, 1], in0=ucp[:, :, 1], in1=fw[:, :, 0], op=ALU.mult),
              reads=uck + ["fcw", "corr"], writes=["corr"])
        silu_q = []

        def emit_silu():
            tc_, ch = silu_q.pop(0)
            S.add("act", lambda e, tc_=tc_, ch=ch: e.activation(out=actT[:, ch, 0:TW], in_=tmp[tc_][:, 0:TW], func=AF.Silu),
                  reads=[("tmp", tc_)], writes=[("actT", ch)])

        for si in range(11):
            sl = load_slab(wup_b[:, si * 512:(si + 1) * 512], KC, 512, "wup")
            for c4 in range(4):
                ch = si * 4 + c4
                b = fm_group(sl, KC, c4 * 128, lambda kc: hT[:, kc, 0:TW], hT_keys, TW)
                tc_ = next_tmp()
                w2_, w1_, w0_ = (fcw[:, ch * 3 + 2:ch * 3 + 3], fcw[:, ch * 3 + 1:ch * 3 + 2], fcw[:, ch * 3:ch * 3 + 1])
                S.add("act", lambda e, b=b, ch=ch: e.activation(out=uc[:, par, ch, :], in_=ps[b][:, TW - 2:TW], func=AF.Copy),
                      reads=[("ps", b)], writes=[("uc", par, ch)])
                S.add("act", lambda e, tc_=tc_, b=b, ch=ch, w2_=w2_: e.activation(
                    out=tmp[tc_][:, 0:TW], in_=ps[b][:, 0:TW], func=AF.Identity, scale=w2_, bias=fcb[:, ch:ch + 1]),
                    reads=[("ps", b), "fcw", "fcb", ("uc", par, ch)], writes=[("tmp", tc_)])
                S.add("dve", lambda e, tc_=tc_, b=b, w1_=w1_: e.scalar_tensor_tensor(
                    out=tmp[tc_][:, 1:TW], in0=ps[b][:, 0:TW - 1], scalar=w1_, in1=tmp[tc_][:, 1:TW],
                    op0=ALU.mult, op1=ALU.add), reads=[("ps", b), ("tmp", tc_), "fcw"], writes=[("tmp", tc_)])
                S.add("dve", lambda e, tc_=tc_, b=b, w0_=w0_: e.scalar_tensor_tensor(
                    out=tmp[tc_][:, 2:TW], in0=ps[b][:, 0:TW - 2], scalar=w0_, in1=tmp[tc_][:, 2:TW],
                    op0=ALU.mult, op1=ALU.add), reads=[("ps", b), ("tmp", tc_), "fcw"], writes=[("tmp", tc_)])
                S.add("pool", lambda e, tc_=tc_, ch=ch: e.tensor_tensor(
                    out=tmp[tc_][:, 0:2], in0=tmp[tc_][:, 0:2], in1=corr[:, ch, :], op=ALU.add),
                    reads=["corr", ("tmp", tc_)], writes=[("tmp", tc_)])
                if ch < 22:
                    silu_q.append((tc_, ch))
                    if len(silu_q) > 2:
                        emit_silu()
                else:
                    while silu_q:
                        emit_silu()
                    S.add("pool", lambda e, tc_=tc_, ch=ch: e.tensor_tensor(
                        out=actT[:, ch - 22, 0:TW], in0=actT[:, ch - 22, 0:TW], in1=tmp[tc_][:, 0:TW], op=ALU.mult),
                        reads=[("tmp", tc_), ("actT", ch - 22)], writes=[("actT", ch - 22)])
            release(sl)
        ak = [("actT", c) for c in range(22)]
        for hf in range(2):
            banks = [next_ps() for _ in range(4)]
            for k0 in (0, 8, 16):
                kn = min(8, 22 - k0)
                sl = load_slab(wdn_b[k0 * 128:(k0 + kn) * 128, hf * 512:(hf + 1) * 512], kn, 512, "wdn")
                for s_ in range(NS):
                    for kk in range(kn):
                        kc = k0 + kk
                        S.add("pe", lambda e, b=banks[s_], kc=kc, kk=kk, s_=s_, sl=sl: e.matmul(
                            ps[b], lhsT=actT[:, kc, s_ * 128:(s_ + 1) * 128], rhs=sl[0][:, kk, :],
                            start=(kc == 0), stop=(kc == 21), skip_group_check=True),
                            reads=[("ws", sl[1]), ("actT", kc)], writes=[("ps", banks[s_])])
                release(sl)
            for s_ in range(NS):
                S.add("dve", lambda e, b=banks[s_], s_=s_, hf=hf: e.tensor_tensor(
                    out=xsrc[:, s_, hf * 512:(hf + 1) * 512], in0=ps[b], in1=xsrc[:, s_, hf * 512:(hf + 1) * 512],
                    op=ALU.add), reads=[("ps", banks[s_]), xk(s_)], writes=[xk(s_)])
            if hf == 0 and mid_hook is not None:
                mid_hook()
        norm_stats(lambda s_: xsrc[:, s_, :], NS, 8, xk)
        for s_ in range(4):
            row = 512 * j + 128 * s_ - 128
            if row < 0 or row >= NOUT:
                continue
            S.add("dve", lambda e, s_=s_: e.scalar_tensor_tensor(
                out=xsrc[:, s_, :], in0=xsrc[:, s_, :], scalar=rstd[:, 8 + s_:9 + s_], in1=g3b,
                op0=ALU.mult, op1=ALU.mult), reads=[xk(s_), ("rstd", 8 + s_), "g3b"], writes=[xk(s_)])
            pending_stores.append(lambda s_=s_, row=row: S.add(
                "sp", lambda e: e.dma_start(out=out_d[row:row + 128, :], in_=xsrc[:, s_, :]),
                reads=[xk(s_)], dma_sem=sem_out))

    S.add("sp", lambda e: e.dma_start(out=xt[1], in_=x_loc[1536:2048, :].rearrange("(s p) d -> p s d", p=128)),
          writes=[("xt", 1, s_) for s_ in range(4)], dma_sem=sem_xt[1])
    xt_load(0)
    ul_load(0)
    n1(-1)
    tile(-1)
    n1(0)
    for j in range(0, ntile):
        tile(j, mid_hook=(lambda j=j: n1(j + 1)) if j + 1 < ntile else None)
    flush_stores()

    final_wait = []
    for sem in (sem_out, sem_dbg, sem_ul):
        if S.dma_count.get(sem, 0):
            final_wait.append((sem, S.dma_count[sem]))

    eng_sems = {e: [] for e in COMPUTE}
    for e in COMPUTE:
        nsig = sum(1 for op in S.ops[e] if op.signal and op.dma_sem is None)
        for i in range(nsig // SEM_LIMIT + 1):
            eng_sems[e].append(newsem("s_%s%d" % (e, i)))
        c = 0
        for op in S.ops[e]:
            if op.dma_sem is None and op.signal:
                op.ev = (eng_sems[e][c // SEM_LIMIT], c % SEM_LIMIT + 1)
                c += 1
        print("engine", e, "ops", len(S.ops[e]), "signals", nsig)
    print("sp ops", len(S.ops["sp"]))

    def lower(ename, tail=None):
        acts, waited = [], {}
        for op in S.ops[ename]:
            for d in op.deps:
                sem, val = d.ev
                if waited.get(id(sem), 0) < val:
                    acts.append(("wait", sem, val, op))
                    waited[id(sem)] = val
            if op.dma_sem is not None:
                acts.append(("op", op.dma_sem, 16, op))
            elif op.signal:
                acts.append(("op", op.ev[0], 1, op))
            else:
                acts.append(("op", None, 0, op))
        for sem, val in (tail or []):
            acts.append(("wait", sem, val, None))
        return acts

    streams = {e: lower(e, final_wait if e == "sp" else None) for e in S.ops}
    semv = {}
    pos = {e: 0 for e in streams}
    progress = True
    while progress:
        progress = False
        for e, acts in streams.items():
            while pos[e] < len(acts):
                a = acts[pos[e]]
                if a[0] == "wait":
                    if semv.get(id(a[1]), 0) < a[2]:
                        break
                elif a[1] is not None:
                    semv[id(a[1])] = semv.get(id(a[1]), 0) + a[2]
                pos[e] += 1
                progress = True
    for e, acts in streams.items():
        if pos[e] < len(acts):
            a = acts[pos[e]]
            own = [q for q, ss in eng_sems.items() if any(x is a[1] for x in ss)]
            print("DEADLOCK: queue", e, "stuck at", pos[e], "/", len(acts), a[0], "need", a[2], "have", semv.get(id(a[1]), 0),
                  "op idx", getattr(a[3], "idx", None), "sem of", own, "deps", [(d.eng, d.idx, d.ev[1]) for d in (a[3].deps if a[3] else [])])
    if any(pos[e] < len(streams[e]) for e in streams):
        raise RuntimeError("semaphore protocol deadlock")
    print("protocol check ok")

    block = es.enter_context(nc.Block())

    def emit_stream(ename, handle, tail=None):
        waited = {}
        for op in S.ops[ename]:
            for d in op.deps:
                sem, val = d.ev
                if waited.get(id(sem), 0) < val:
                    handle.wait_ge(sem, val)
                    waited[id(sem)] = val
            ins = op.fn(handle)
            if op.dma_sem is not None:
                ins.then_inc(op.dma_sem, 16)
            elif op.signal:
                ins.then_inc(op.ev[0], 1)
        if tail:
            for sem, val in tail:
                handle.wait_ge(sem, val)

    @block.tensor
    def _(pe):
        emit_stream("pe", pe)

    @block.scalar
    def _(act):
        emit_stream("act", act)

    @block.vector
    def _(dve):
        emit_stream("dve", dve)

    @block.gpsimd
    def _(pool):
        emit_stream("pool", pool)

    @block.sync
    def _(sp):
        emit_stream("sp", sp, tail=final_wait)

    es.close()
    return nc


def _prep_core(c, x, consts):
    b, hf = c // 2, c % 2
    t0 = hf * 4096
    lo = t0 - OUT0
    xl = np.zeros((NLOC, D), np.float32)
    s0, s1 = max(lo, 0), min(lo + NLOC, 8192)
    xl[s0 - lo:s1 - lo] = x[b, s0:s1]
    tok_valid = np.ones(NLOC + 2048, np.float32)
    if hf == 0:
        tok_valid[:OUT0] = 0.0
    i = np.arange(128)
    vld0 = np.zeros((128, 40), np.float32)
    vld1 = np.zeros((128, 40), np.float32)
    for j in range(-1, 9):
        for k in range(4):
            col = (j + 1) * 4 + k
            vld0[:, col] = tok_valid[2048 + 512 * j + 128 * k + i]
            vld1[:, col] = tok_valid[2048 + 512 * j + 4 * i + k]
    vld2 = np.zeros((128, 64), np.float32)
    for s in range(4):
        for r in range(16):
            vld2[:, s * 16 + r] = tok_valid[2048 * s + 16 * i + r]
    perm = np.zeros((128, 128), np.float32)
    t = np.arange(128)
    perm[t, (t % 16) * 8 + t // 16] = 1.0
    m = dict(consts)
    m.update(x_loc=xl, vld0=vld0, vld1=vld1, vld2=vld2, perm16=perm)
    return m


_NC_CACHE = {}


def kernel(x, norm_mix_g, w_in, b_gate, conv_a_w, conv_a_b, w_proj_a, w_proj_b, w_out, norm_ffn_g,
           w_up, ffn_conv_w, ffn_conv_b, w_down, final_norm_g, _dbg=None):
    f = lambda a: np.ascontiguousarray(np.asarray(a, dtype=np.float32))
    x = f(x)
    consts = dict(
        w_in=f(w_in)[0], w_pa=f(w_proj_a)[0], w_pb=f(w_proj_b)[0], w_out=f(w_out)[0], w_up=f(w_up)[0],
        w_dn=f(w_down)[0],
        g1=f(norm_mix_g).reshape(1, D), g2=f(norm_ffn_g).reshape(1, D), g3=f(final_norm_g).reshape(1, D),
        bg=f(f(b_gate)[0].reshape(2, 8, 128).transpose(2, 0, 1).reshape(128, 16)),
        caw=f(f(conv_a_w)[0].reshape(3, 4, 128).transpose(2, 1, 0).reshape(128, 12)),
        cab=f(f(conv_a_b)[0].reshape(4, 128).T),
        fcw=f(f(ffn_conv_w)[0].reshape(3, 44, 128).transpose(2, 1, 0).reshape(128, 132)),
        fcb=f(f(ffn_conv_b)[0].reshape(44, 128).T),
    )
    key = repr(sorted((_dbg or {}).items()))
    if key not in _NC_CACHE:
        _NC_CACHE[key] = build_program(_dbg)
    nc = _NC_CACHE[key]
    in_maps = [_prep_core(c, x, consts) for c in range(8)]
    res = run_bass_kernel_spmd(nc, in_maps, core_ids=list(range(8)))
    out = np.empty((4, 8192, D), np.float32)
    for c in range(8):
        out[c // 2, (c % 2) * 4096:(c % 2 + 1) * 4096] = res.results[c]["out"]
    if _dbg:
        return out, res
    return out
```
